# Optimizing a Trainium2 kernel written in Bass

```python
import jax, jax.numpy as jnp
from jax import lax
import numpy as np

D_MODEL = 2048
BATCH = 4
SEQ = 2048
DEPTH = 2

CTX_LEN = 256
GRID_W = 64
N_MIXERS = 2
RET_HEADS = 8
RET_QK_DIM = D_MODEL // RET_HEADS
RET_V_DIM = 2 * D_MODEL // RET_HEADS
RET_CHUNK = 128
ATT_HEADS = 16
ATT_KV_HEADS = 4
ATT_HEAD_DIM = D_MODEL // ATT_HEADS
ATT_BLOCK = 128
FFN_DIM = 256 * ((8 * D_MODEL // 3 + 255) // 256)
CONV_WIDTH = 3
ROPE_THETA = 10000.0
EPS = 1e-6

kernel_name = 'hybrid_retention_gqa_dit'


def rms_norm(x, w):
    xf = x.astype(jnp.float32)
    y = xf * lax.rsqrt(jnp.mean(xf * xf, axis=-1, keepdims=True) + EPS)
    return (y * w.astype(jnp.float32)).astype(x.dtype)


def modulate(h, shift, scale):
    return h * (1.0 + scale) + shift


def axial_rope_tables(n, head_dim):
    rows = n // GRID_W
    row = jnp.repeat(jnp.arange(rows, dtype=jnp.float32), GRID_W)
    col = jnp.tile(jnp.arange(GRID_W, dtype=jnp.float32), rows)
    n_freq = head_dim // 4
    inv = ROPE_THETA ** (-jnp.arange(n_freq, dtype=jnp.float32) / n_freq)
    ang = jnp.concatenate([row[:, None] * inv, col[:, None] * inv], axis=-1)
    return jnp.cos(ang), jnp.sin(ang)


def apply_rope(x, cos, sin):
    xf = x.astype(jnp.float32).reshape(*x.shape[:-1], -1, 2)
    x1, x2 = xf[..., 0], xf[..., 1]
    c = cos[None, :, None, :]
    s = sin[None, :, None, :]
    out = jnp.stack([x1 * c - x2 * s, x1 * s + x2 * c], axis=-1).reshape(x.shape)
    return out.astype(x.dtype)


def retention_scan(q, k, v, log_gamma, state0):
    bsz, heads, length, _ = q.shape
    dv = v.shape[-1]
    n_chunks = length // RET_CHUNK
    idx = jnp.arange(RET_CHUNK, dtype=jnp.float32)
    lg = log_gamma[:, None]
    diff = idx[:, None] - idx[None, :]
    lower = diff >= 0
    intra = jnp.where(lower, jnp.exp(lg[:, :, None] * jnp.where(lower, diff, 0.0)), 0.0)
    q_decay = jnp.exp(lg * (idx + 1.0))[:, :, None]
    k_decay = jnp.exp(lg * (RET_CHUNK - 1.0 - idx))[:, :, None]
    chunk_decay = jnp.exp(log_gamma * RET_CHUNK)[:, None, None]

    def to_chunks(t):
        return jnp.moveaxis(t.reshape(bsz, heads, n_chunks, RET_CHUNK, t.shape[-1]), 2, 0)

    def step(state, qkv):
        qc, kc, vc = qkv
        scores = jnp.einsum('bhnd,bhmd->bhnm', qc, kc) * intra
        out = jnp.einsum('bhnm,bhme->bhne', scores, vc) + jnp.einsum('bhnd,bhde->bhne', qc, state) * q_decay
        state = state * chunk_decay + jnp.einsum('bhmd,bhme->bhde', kc * k_decay, vc)
        return state, out

    state, out = lax.scan(step, state0, (to_chunks(q), to_chunks(k), to_chunks(v)))
    out = jnp.moveaxis(out, 0, 2).reshape(bsz, heads, length, dv)
    return out, state


def retention_mixer(hx, hc, w_in, w_out, log_decay, gn_w, cos, sin, need_ctx):
    d = D_MODEL

    def project(h):
        bsz, length = h.shape[:2]
        q, k, v, g = jnp.split(h @ w_in, [d, 2 * d, 4 * d], axis=-1)
        q = q.reshape(bsz, length, RET_HEADS, RET_QK_DIM)
        k = k.reshape(bsz, length, RET_HEADS, RET_QK_DIM) * (RET_QK_DIM ** -0.5)
        v = v.reshape(bsz, length, RET_HEADS, RET_V_DIM)
        return q, k, v, g

    def to_heads(t):
        return jnp.swapaxes(t, 1, 2).astype(jnp.float32)

    def flip(t):
        return t[:, :, ::-1]

    qx, kx, vx, gx = project(hx)
    qc, kc, vc, gc = project(hc)
    qx, kx = apply_rope(qx, cos, sin), apply_rope(kx, cos, sin)
    qx, kx, vx = to_heads(qx), to_heads(kx), to_heads(vx)
    qc, kc, vc = to_heads(qc), to_heads(kc), to_heads(vc)
    log_gamma = -jnp.exp(log_decay.astype(jnp.float32))
    zeros = jnp.zeros((hx.shape[0], RET_HEADS, RET_QK_DIM, RET_V_DIM), jnp.float32)
    oc_f, st_f = retention_scan(qc, kc, vc, log_gamma[0], zeros)
    oc_b, st_b = retention_scan(flip(qc), flip(kc), flip(vc), log_gamma[1], zeros)
    ox_f, _ = retention_scan(qx, kx, vx, log_gamma[0], st_f)
    ox_b, _ = retention_scan(flip(qx), flip(kx), flip(vx), log_gamma[1], st_b)

    def finish(y, g, dtype):
        bsz, _, length, _ = y.shape
        mu = jnp.mean(y, axis=-1, keepdims=True)
        var = jnp.mean(jnp.square(y - mu), axis=-1, keepdims=True)
        y = (y - mu) * lax.rsqrt(var + EPS)
        y = jnp.swapaxes(y, 1, 2).reshape(bsz, length, RET_HEADS * RET_V_DIM) * gn_w.astype(jnp.float32)
        return (jax.nn.silu(g.astype(jnp.float32)) * y).astype(dtype) @ w_out

    out_x = finish(ox_f + flip(ox_b), gx, hx.dtype)
    out_c = finish(oc_f + flip(oc_b), gc, hc.dtype) if need_ctx else None
    return out_x, out_c


def attend(q, k, v):
    bsz, lq = q.shape[:2]
    groups = ATT_HEADS // ATT_KV_HEADS
    n_blocks = lq // ATT_BLOCK
    qb = jnp.moveaxis(q.reshape(bsz, n_blocks, ATT_BLOCK, ATT_KV_HEADS, groups, ATT_HEAD_DIM), 1, 0)
    scale = ATT_HEAD_DIM ** -0.5

    def block(qi):
        s = jnp.einsum('bqkgd,btkd->bkgqt', qi, k).astype(jnp.float32) * scale
        p = jax.nn.softmax(s, axis=-1).astype(v.dtype)
        return jnp.einsum('bkgqt,btkd->bqkgd', p, v)

    o = lax.map(block, qb)
    return jnp.moveaxis(o, 0, 1).reshape(bsz, lq, ATT_HEADS * ATT_HEAD_DIM)


def gqa_mixer(hx, hc, w_in, w_out, q_norm, k_norm, cos, sin, need_ctx):
    q_w = ATT_HEADS * ATT_HEAD_DIM
    kv_w = ATT_KV_HEADS * ATT_HEAD_DIM

    def project(h):
        bsz, length = h.shape[:2]
        q, k, v = jnp.split(h @ w_in, [q_w, q_w + kv_w], axis=-1)
        q = rms_norm(q.reshape(bsz, length, ATT_HEADS, ATT_HEAD_DIM), q_norm)
        k = rms_norm(k.reshape(bsz, length, ATT_KV_HEADS, ATT_HEAD_DIM), k_norm)
        v = v.reshape(bsz, length, ATT_KV_HEADS, ATT_HEAD_DIM)
        return q, k, v

    qx, kx, vx = project(hx)
    qc, kc, vc = project(hc)
    qx, kx = apply_rope(qx, cos, sin), apply_rope(kx, cos, sin)
    k_all = jnp.concatenate([kc, kx], axis=1)
    v_all = jnp.concatenate([vc, vx], axis=1)
    out_x = attend(qx, k_all, v_all) @ w_out
    out_c = attend(qc, kc, vc) @ w_out if need_ctx else None
    return out_x, out_c


def conv_ffn(h, w_up, conv_w, conv_b, w_down):
    length = h.shape[1]
    u = h @ w_up
    pad = CONV_WIDTH // 2
    up = jnp.pad(u, ((0, 0), (pad, pad), (0, 0)))
    u = sum(up[:, j:j + length] * conv_w[j] for j in range(CONV_WIDTH)) + conv_b
    a, b = jnp.split(u, 2, axis=-1)
    return (jax.nn.silu(a) * b) @ w_down


def setup_inputs(seed: int = 0) -> dict:
    key = jax.random.key(seed)
    ks = jax.random.split(key, 24)
    d = D_MODEL
    n_ret = (DEPTH + 1) // 2
    n_att = DEPTH // 2
    f32 = jnp.float32

    def nrm(k, shape, scale):
        return jax.random.normal(k, shape, f32) * scale

    decay_rate = jnp.exp2(-5.0 - jnp.arange(RET_HEADS, dtype=f32))
    base_log_decay = jnp.log(-jnp.log1p(-decay_rate))
    att_in = (ATT_HEADS + 2 * ATT_KV_HEADS) * ATT_HEAD_DIM
    return {
        'x': nrm(ks[0], (BATCH, SEQ, d), 1.0),
        'c': nrm(ks[1], (BATCH, d), 1.0),
        'ctx': nrm(ks[2], (BATCH, CTX_LEN, d), 1.0),
        'c_ctx': nrm(ks[3], (d,), 1.0),
        'ada_w': nrm(ks[4], (DEPTH, d, 6 * d), d ** -0.5),
        'ada_b': nrm(ks[5], (DEPTH, 6 * d), 0.02),
        'norm_w': 1.0 + nrm(ks[6], (DEPTH, 2, d), 0.02),
        'ret_w_in': nrm(ks[7], (n_ret, d, 6 * d), d ** -0.5),
        'ret_w_out': nrm(ks[8], (n_ret, 2 * d, d), (2 * d) ** -0.5),
        'ret_log_decay': base_log_decay[None, None, :] + nrm(ks[9], (n_ret, 2, RET_HEADS), 0.1),
        'ret_gn_w': 1.0 + nrm(ks[10], (n_ret, 2 * d), 0.02),
        'attn_w_in': nrm(ks[11], (n_att, d, att_in), d ** -0.5),
        'attn_w_out': nrm(ks[12], (n_att, d, d), d ** -0.5),
        'attn_q_norm': 1.0 + nrm(ks[13], (n_att, ATT_HEAD_DIM), 0.02),
        'attn_k_norm': 1.0 + nrm(ks[14], (n_att, ATT_HEAD_DIM), 0.02),
        'ffn_w_up': nrm(ks[15], (DEPTH, d, 2 * FFN_DIM), d ** -0.5),
        'ffn_conv_w': nrm(ks[16], (DEPTH, CONV_WIDTH, 2 * FFN_DIM), CONV_WIDTH ** -0.5),
        'ffn_conv_b': nrm(ks[17], (DEPTH, 2 * FFN_DIM), 0.02),
        'ffn_w_down': nrm(ks[18], (DEPTH, FFN_DIM, d), FFN_DIM ** -0.5),
        'final_norm_w': 1.0 + nrm(ks[19], (d,), 0.02),
    }


def reference(x, c, ctx, c_ctx, ada_w, ada_b, norm_w, ret_w_in, ret_w_out, ret_log_decay, ret_gn_w,
              attn_w_in, attn_w_out, attn_q_norm, attn_k_norm, ffn_w_up, ffn_conv_w, ffn_conv_b,
              ffn_w_down, final_norm_w):
    n_tok = x.shape[1]
    cos_r, sin_r = axial_rope_tables(n_tok, RET_QK_DIM)
    cos_a, sin_a = axial_rope_tables(n_tok, ATT_HEAD_DIM)
    c_act = jax.nn.silu(c)
    cc_act = jax.nn.silu(c_ctx)
    for i in range(DEPTH):
        last = i == DEPTH - 1
        j = i // N_MIXERS
        mod_x = (c_act @ ada_w[i] + ada_b[i])[:, None, :]
        mod_c = cc_act @ ada_w[i] + ada_b[i]
        sh1x, sc1x, g1x, sh2x, sc2x, g2x = jnp.split(mod_x, 6, axis=-1)
        sh1c, sc1c, g1c, sh2c, sc2c, g2c = jnp.split(mod_c, 6, axis=-1)
        hx = modulate(rms_norm(x, norm_w[i, 0]), sh1x, sc1x)
        hc = modulate(rms_norm(ctx, norm_w[i, 0]), sh1c, sc1c)
        if i % N_MIXERS == 0:
            ox, oc = retention_mixer(hx, hc, ret_w_in[j], ret_w_out[j], ret_log_decay[j], ret_gn_w[j],
                                     cos_r, sin_r, not last)
        else:
            ox, oc = gqa_mixer(hx, hc, attn_w_in[j], attn_w_out[j], attn_q_norm[j], attn_k_norm[j],
                               cos_a, sin_a, not last)
        x = x + g1x * ox
        hx = modulate(rms_norm(x, norm_w[i, 1]), sh2x, sc2x)
        x = x + g2x * conv_ffn(hx, ffn_w_up[i], ffn_conv_w[i], ffn_conv_b[i], ffn_w_down[i])
        if not last:
            ctx = ctx + g1c * oc
            hc = modulate(rms_norm(ctx, norm_w[i, 1]), sh2c, sc2c)
            ctx = ctx + g2c * conv_ffn(hc, ffn_w_up[i], ffn_conv_w[i], ffn_conv_b[i], ffn_w_down[i])
    return rms_norm(x, final_norm_w)
```

```python
import math
from contextlib import ExitStack

import numpy as np
import concourse.bass as bass
import concourse.mybir as mybir
from concourse.bass_utils import run_bass_kernel_spmd

F32 = mybir.dt.float32
BF16 = mybir.dt.bfloat16
ALU = mybir.AluOpType
AF = mybir.ActivationFunctionType
EPS = 1e-6
SM_BIAS = -12.0


class Cfg:
    def __init__(self, D=2048, SEQ=2048, CTX=256, BATCH=4):
        self.D, self.SEQ, self.CTX, self.BATCH = D, SEQ, CTX, BATCH
        self.KC = D // 128
        self.RH = D // 256
        self.AH = D // 128
        self.AKV = self.AH // 4
        self.KVW = self.AKV * 128
        self.FF = 256 * ((8 * D // 3 + 255) // 256)
        self.FC = self.FF // 128
        self.TOT = SEQ + CTX + 4
        self.NT = SEQ + CTX
        self.NCH = self.NT // 128
        self.NB = 6 * D // 512
        self.GRID_W = 64
        self.HALF = SEQ // 2
        self.NQ = SEQ // 2 + 2


class Buf:
    __slots__ = ("name", "w", "r")

    def __init__(self, name=""):
        self.name = name
        self.w = {}
        self.r = {}


class Sched:
    def __init__(self, nc, es, n_slots=8):
        self.nc = nc
        self.eng = {}
        for name, h in (("pe", nc.tensor), ("act", nc.scalar), ("dve", nc.vector),
                        ("pool", nc.gpsimd), ("sp", nc.sync)):
            sem = es.enter_context(nc.semaphore("sem_" + name))
            self.eng[name] = dict(h=h, sem=sem, cnt=0, waited={}, pend=False)
        self.slots = {}
        for q in ("sp", "pool"):
            sl = []
            for i in range(n_slots):
                sem = es.enter_context(nc.semaphore(f"dq_{q}{i}"))
                sl.append([sem, 0])
            self.slots[q] = dict(sl=sl, nxt=0)
        self.n_ins = 0

    def buf(self, name=""):
        return Buf(name)

    def bufs(self, n, name=""):
        return [Buf(f"{name}{i}") for i in range(n)]

    def _wait(self, eng, deps):
        E = self.eng[eng]
        best = {}
        for (sem, val, src) in deps:
            if src == eng and eng == "pe":
                continue
            k = id(sem)
            if E["waited"].get(k, 0) >= val:
                continue
            if k not in best or best[k][1] < val:
                best[k] = (sem, val)
        for k, (sem, val) in best.items():
            E["h"].wait_ge(sem, val)
            E["waited"][k] = val
            self.n_ins += 1

    @staticmethod
    def _deps(reads, writes):
        deps = []
        for b in reads:
            deps.extend(b.w.values())
        for b in writes:
            deps.extend(b.w.values())
            deps.extend(b.r.values())
        return deps

    @staticmethod
    def _commit(ev, reads, writes):
        k = id(ev[0])
        for b in writes:
            if k not in b.w or b.w[k][1] < ev[1]:
                b.w[k] = ev
            b.r = {}
        for b in reads:
            if k not in b.r or b.r[k][1] < ev[1]:
                b.r[k] = ev

    def op(self, eng, reads, writes, fn, inc=True):
        E = self.eng[eng]
        self._wait(eng, self._deps(reads, writes))
        ins = fn(E["h"])
        self.n_ins += 1
        if inc:
            E["cnt"] += 1
            ins.then_inc(E["sem"], 1)
            ev = (E["sem"], E["cnt"], eng)
            E["pend"] = False
        else:
            ev = (E["sem"], E["cnt"] + 1, eng)
            E["pend"] = True
        self._commit(ev, reads, writes)
        return ev

    def dma(self, q, out, in_, reads, writes):
        E = self.eng[q]
        SL = self.slots[q]
        slot = SL["sl"][SL["nxt"]]
        SL["nxt"] = (SL["nxt"] + 1) % len(SL["sl"])
        deps = self._deps(reads, writes)
        if slot[1] > 0:
            deps.append((slot[0], slot[1], "dma"))
        self._wait(q, deps)
        ins = E["h"].dma_start(out=out, in_=in_)
        self.n_ins += 1
        slot[1] += 16
        ins.then_inc(slot[0], 16)
        ev = (slot[0], slot[1], "dma")
        self._commit(ev, reads, writes)
        return ev

    def barrier(self):
        evs = []
        for n, E in self.eng.items():
            assert not E["pend"], n
            if E["cnt"] > 0:
                evs.append((E["sem"], E["cnt"], n))
        for q, SL in self.slots.items():
            for sem, val in SL["sl"]:
                if val > 0:
                    evs.append((sem, val, "dma"))
        for n in self.eng:
            self._wait(n, [e for e in evs if e[2] != n or n != "pe"])


def _rope_tables(n, head_dim, grid_w):
    rows = n // grid_w
    row = np.repeat(np.arange(rows, dtype=np.float32), grid_w)
    col = np.tile(np.arange(grid_w, dtype=np.float32), rows)
    n_freq = head_dim // 4
    inv = (np.float32(10000.0) ** (-np.arange(n_freq, dtype=np.float32) / np.float32(n_freq))).astype(np.float32)
    ang = np.concatenate([row[:, None] * inv, col[:, None] * inv], axis=-1).astype(np.float32)
    return np.cos(ang).astype(np.float32), np.sin(ang).astype(np.float32)


def make_consts(cfg):
    c = {}
    c["ident"] = np.eye(128, dtype=np.float32)
    c["ones"] = np.ones((128, 128), np.float32)
    pt = np.zeros((128, 128), np.float32)
    for i in range(64):
        pt[2 * i + 1, 2 * i] = -1.0
        pt[2 * i, 2 * i + 1] = 1.0
    c["protT"] = pt
    m = np.arange(128, dtype=np.float32)[:, None]
    n = np.arange(128, dtype=np.float32)[None, :]
    mk = np.zeros((128, 4, 128), np.float32)
    mk[:, 0] = np.maximum(n - m, 0)
    mk[:, 1] = (n >= m)
    mk[:, 2] = np.maximum(m - n, 0)
    mk[:, 3] = (m >= n)
    c["maskc"] = mk
    p = np.arange(128, dtype=np.float32)
    c["idxc"] = np.stack([p + 1, 128 - p, 127 - p, p, np.full(128, 128.0, np.float32)], 1).astype(np.float32)
    cr, sr = _rope_tables(cfg.SEQ, 256, cfg.GRID_W)
    f = np.arange(256)
    tab = np.stack([cr[:, f // 2].T, sr[:, f // 2].T], 0)
    c["ropeR"] = np.ascontiguousarray(tab.reshape(2, 2, 128, cfg.SEQ).transpose(0, 2, 1, 3))
    ca, sa = _rope_tables(cfg.SEQ, 128, cfg.GRID_W)
    f = np.arange(128)
    c["ropeA"] = np.ascontiguousarray(np.stack([ca[:, f // 2].T, sa[:, f // 2].T], 0))
    return c


def prep_weights(cfg, I):
    KC, FC, RH, AKV, D, FF = cfg.KC, cfg.FC, cfg.RH, cfg.AKV, cfg.D, cfg.FF
    f32 = lambda a: np.ascontiguousarray(np.asarray(a, dtype=np.float32))
    W = {}
    W["adaW"] = f32(np.asarray(I["ada_w"]).reshape(2, KC, 128, cfg.NB, 512).transpose(0, 3, 2, 1, 4))
    W["adaB"] = f32(np.asarray(I["ada_b"]).reshape(2, 6 * KC, 128).transpose(2, 0, 1))
    W["normW"] = f32(np.asarray(I["norm_w"]).reshape(2, 2, KC, 128).transpose(3, 0, 1, 2))
    W["fnW"] = f32(np.asarray(I["final_norm_w"]).reshape(KC, 128).T)
    rw = np.asarray(I["ret_w_in"])[0]
    q, k, v, g = rw[:, 0:D], rw[:, D:2 * D], rw[:, 2 * D:4 * D], rw[:, 4 * D:6 * D]
    blocks = []
    for h in range(RH):
        qk = np.concatenate([q[:, 256 * h:256 * h + 256], k[:, 256 * h:256 * h + 256]], 1)
        blocks.append(np.stack([qk, v[:, 512 * h:512 * h + 512], g[:, 512 * h:512 * h + 512]], 0))
    rb = np.stack(blocks, 0)
    W["retWin"] = f32(rb.reshape(RH, 3, KC, 128, 512).transpose(0, 1, 3, 2, 4))
    W["retWout"] = f32(np.asarray(I["ret_w_out"])[0].reshape(2 * KC, 128, KC, 128).transpose(2, 1, 0, 3))
    W["retLD"] = f32(np.asarray(I["ret_log_decay"])[0].reshape(1, 2 * RH))
    W["retGN"] = f32(np.asarray(I["ret_gn_w"])[0].reshape(1, 2 * D))
    aw = np.asarray(I["attn_w_in"])[0]
    W["attWq"] = f32(aw[:, 0:D].reshape(KC, 128, AKV, 512).transpose(2, 1, 0, 3))
    W["attWk"] = f32(aw[:, D:D + cfg.KVW].reshape(KC, 128, cfg.KVW).transpose(1, 0, 2))
    W["attWv"] = f32(aw[:, D + cfg.KVW:D + 2 * cfg.KVW].reshape(KC, 128, cfg.KVW).transpose(1, 0, 2))
    W["attWout"] = f32(np.asarray(I["attn_w_out"])[0].reshape(KC, 128, KC, 128).transpose(2, 1, 0, 3))
    W["attQN"] = f32(np.asarray(I["attn_q_norm"])[0].reshape(128, 1))
    W["attKN"] = f32(np.asarray(I["attn_k_norm"])[0].reshape(128, 1))
    W["ffnUp"] = f32(np.asarray(I["ffn_w_up"]).reshape(2, KC, 128, 2, FC, 128).transpose(0, 4, 2, 1, 3, 5)
                     .reshape(2, FC, 128, KC, 256))
    W["convW"] = f32(np.asarray(I["ffn_conv_w"]).reshape(2, 3, 2, FC, 128).transpose(4, 0, 3, 2, 1))
    W["convB"] = f32(np.asarray(I["ffn_conv_b"]).reshape(2, 2, FC, 128).transpose(3, 0, 2, 1))
    W["ffnDown"] = f32(np.asarray(I["ffn_w_down"]).reshape(2, FC, 128, KC, 128).transpose(0, 3, 2, 1, 4))
    return W


def prep_core(cfg, I, b):
    KC, SEQ, CTX, D = cfg.KC, cfg.SEQ, cfg.CTX, cfg.D
    xT = np.zeros((D, cfg.TOT), np.float32)
    xT[:, 1:SEQ + 1] = np.asarray(I["x"])[b].T
    xT[:, SEQ + 3:SEQ + 3 + CTX] = np.asarray(I["ctx"])[b].T
    m = {}
    m["xT"] = np.ascontiguousarray(xT.reshape(KC, 128, cfg.TOT).transpose(1, 0, 2))
    cT = np.stack([np.asarray(I["c"])[b], np.asarray(I["c_ctx"])], -1).astype(np.float32)
    m["cT"] = np.ascontiguousarray(cT.reshape(KC, 128, 2).transpose(1, 0, 2))
    return m


def prep_half(cfg, consts, r):
    m = {}
    sel = np.zeros((128, 4), np.float32)
    sel[:, 0] = 1 - r
    sel[:, 1] = r
    sel[:, 2] = r
    sel[:, 3] = 1 - r
    m["rsel"] = sel
    t = np.clip(r * cfg.HALF - 1 + np.arange(cfg.NQ), 0, cfg.SEQ - 1)
    m["ropeAw"] = np.ascontiguousarray(consts["ropeA"][:, :, t])
    return m


INPUT_SHAPES = lambda cfg: dict(
    xT=[128, cfg.KC, cfg.TOT], cT=[128, cfg.KC, 2],
    adaW=[2, cfg.NB, 128, cfg.KC, 512], adaB=[128, 2, 6 * cfg.KC], normW=[128, 2, 2, cfg.KC], fnW=[128, cfg.KC],
    retWin=[cfg.RH, 3, 128, cfg.KC, 512], retWout=[cfg.KC, 128, 2 * cfg.KC, 128], retLD=[1, 2 * cfg.RH],
    retGN=[1, 2 * cfg.D],
    attWq=[cfg.AKV, 128, cfg.KC, 512], attWk=[128, cfg.KC, cfg.KVW], attWv=[128, cfg.KC, cfg.KVW],
    attWout=[cfg.KC, 128, cfg.KC, 128], attQN=[128, 1], attKN=[128, 1],
    ffnUp=[2, cfg.FC, 128, cfg.KC, 256], convW=[128, 2, cfg.FC, 2, 3], convB=[128, 2, cfg.FC, 2],
    ffnDown=[2, cfg.KC, 128, cfg.FC, 128],
    ident=[128, 128], ones=[128, 128], protT=[128, 128], maskc=[128, 4, 128], idxc=[128, 5],
    ropeR=[2, 128, 2, cfg.SEQ], ropeA=[2, 128, cfg.SEQ],
    ropeAw=[2, 128, cfg.NQ], rsel=[128, 4],
)


class Builder:
    def __init__(self, cfg, stop_after=None, debug_x1=False):
        self.cfg = cfg
        self.stop_after = stop_after
        self.debug_x1 = debug_x1
        self.nc = bass.Bass("TRN2", target_bir_lowering=False)
        nc = self.nc
        self.I = {n: nc.dram_tensor(n, s, F32, kind="ExternalInput").ap() for n, s in INPUT_SHAPES(cfg).items()}
        self.outT = nc.dram_tensor("outT", [128, cfg.KC, cfg.HALF], F32, kind="ExternalOutput").ap()
        self.xwT = nc.dram_tensor("xwT", [128, cfg.KC, cfg.NQ], F32, kind="Internal").ap()
        self.hxwT = nc.dram_tensor("hxwT", [128, cfg.KC, cfg.NQ], BF16, kind="Internal").ap()
        if debug_x1:
            self.dbgT = nc.dram_tensor("dbgT", [128, cfg.KC, cfg.TOT], F32, kind="ExternalOutput").ap()
        self.hxT = nc.dram_tensor("hxT", [128, cfg.KC, cfg.TOT], BF16, kind="Internal").ap()
        self.zT = nc.dram_tensor("zT", [128, 2 * cfg.KC, cfg.TOT], BF16, kind="Internal").ap()
        self.x1T = nc.dram_tensor("x1T", [128, cfg.KC, cfg.TOT], F32, kind="Internal").ap()
        self.WB = dict(
            retWout=nc.dram_tensor("retWoutB", [cfg.KC, 128, 2 * cfg.KC, 128], BF16, kind="Internal").ap(),
            attWout=nc.dram_tensor("attWoutB", [cfg.KC, 128, cfg.KC, 128], BF16, kind="Internal").ap(),
            ffnUp=nc.dram_tensor("ffnUpB", [2, cfg.FC, 128, cfg.KC, 256], BF16, kind="Internal").ap(),
            ffnDown=nc.dram_tensor("ffnDownB", [2, cfg.KC, 128, cfg.FC, 128], BF16, kind="Internal").ap(),
        )

    def sb(self, es, name, shape, dt):
        self._uid = getattr(self, "_uid", 0) + 1
        return es.enter_context(self.nc.sbuf_tensor(f"sb{self._uid}_{name}", shape, dt))

    def seq_col(self, s):
        c = self.cfg
        return c.SEQ + 3 + s if s < c.CTX else s - c.CTX + 1

    def make_conv_jobs(self):
        cfg, S = self.cfg, self.S
        self.conv_jobs = []
        self.conv_buf = {}

        def add(kind, idx):
            b = S.buf(f"cv_{kind}{idx}")
            self.conv_buf[(kind,) + idx] = b
            self.conv_jobs.append((self.WB[kind][idx], self.I[kind][idx], b))
        for f in range(cfg.KC):
            add("retWout", (f,))
        for j in range(cfg.FC):
            add("ffnUp", (0, j))
        for f in range(cfg.KC):
            add("ffnDown", (0, f))
        for f in range(cfg.KC):
            add("attWout", (f,))
        for j in range(cfg.FC):
            add("ffnUp", (1, j))
        for f in range(cfg.KC):
            add("ffnDown", (1, f))
        self.conv_total = len(self.conv_jobs)

    def emit_conv(self, n, gate=()):
        for _ in range(n):
            if not self.conv_jobs:
                return
            dst, src, b = self.conv_jobs.pop(0)
            self.S.dma("pool", dst, src, list(gate), [b])

    def build(self):
        cfg, nc = self.cfg, self.nc
        with ExitStack() as es:
            self.S = S = Sched(nc, es)
            self.make_conv_jobs()
            self.PS = [es.enter_context(nc.psum_tensor(f"ps{i}", [128, 512], F32)) for i in range(7)]
            self.PSB = [S.buf(f"ps{i}") for i in range(7)]
            self.PT = es.enter_context(nc.psum_tensor("pst", [128, 1024], BF16))
            self.PTB = [S.buf("pst0"), S.buf("pst1")]
            sb = lambda n, s, d: self.sb(es, n, s, d)
            self.ident = sb("ident", [128, 128], BF16)
            self.ones = sb("ones", [128, 128], BF16)
            self.protT = sb("protT", [128, 128], BF16)
            self.modT = sb("modT", [128, 2, 6 * cfg.KC, 2], F32)
            self.acoef = sb("acoef", [128, 2, 2, 2, cfg.KC], F32)
            self.normW = sb("normW", [128, 2, 2, cfg.KC], F32)
            self.fnW = sb("fnW", [128, cfg.KC], F32)
            self.convW = sb("convW", [128, 2, cfg.FC, 2, 3], F32)
            self.convB = sb("convB", [128, 2, cfg.FC, 2], F32)
            self.epsc = sb("epsc", [128, 1], F32)
            self.B_const = S.buf("const")
            self.B_mod = S.buf("mod")
            self.B_hxT, self.B_zT, self.B_x1T, self.B_out = S.buf("hxT"), S.buf("zT"), S.buf("x1T"), S.buf("out")
            self.B_xwT, self.B_hxwT = S.buf("xwT"), S.buf("hxwT")
            self.rsel = sb("rsel", [128, 4], F32)
            S.dma("sp", self.rsel[:], self.I["rsel"][:, :], [], [self.B_const])
            Bc = self.B_const
            S.dma("pool", self.ident[:], self.I["ident"][:, :], [], [Bc])
            S.dma("pool", self.ones[:], self.I["ones"][:, :], [], [Bc])
            S.dma("pool", self.protT[:], self.I["protT"][:, :], [], [Bc])
            S.dma("sp", self.normW[:], self.I["normW"][:, :, :, :], [], [Bc])
            S.dma("sp", self.fnW[:], self.I["fnW"][:, :], [], [Bc])
            S.dma("sp", self.convW[:], self.I["convW"][:, :, :, :, :], [], [Bc])
            S.dma("sp", self.convB[:], self.I["convB"][:, :, :, :], [], [Bc])
            S.op("dve", [], [Bc], lambda e: e.memset(self.epsc[:], EPS))

            np_ = getattr(self, "nphase", 99)
            self.phase_mods()
            S.barrier()
            if np_ >= 2:
                self.phase_norm1(0, self.I["xT"], None)
                S.barrier()
            if np_ >= 3:
                self.phase_retention()
                S.barrier()
            if np_ >= 4:
                self.phase_tokens(0)
                S.barrier()
            if self.stop_after != 0:
                self.phase_norm1(1, self.x1T, self.B_x1T)
                S.barrier()
                import os
                self.phase_window()
                S.barrier()
                if os.environ.get("DBG_STOPW") != "1":
                    self.phase_attention()
                    S.barrier()
                    if os.environ.get("DBG_STOPW") != "2":
                        self.phase_tokens(1)
                        S.barrier()
            S._wait("sp", list(self.B_out.w.values()))
        return nc

    def phase_mods(self):
        cfg, nc, S = self.cfg, self.nc, self.S
        KC = cfg.KC
        with ExitStack() as es:
            sb = lambda n, s, d: self.sb(es, n, s, d)
            c32 = sb("c32", [128, KC, 2], F32)
            cact = sb("cact", [128, KC, 2], BF16)
            adab = sb("adab", [128, 2, 6 * KC], F32)
            wsl = [sb(f"adaw{i}", [128, KC, 512], BF16) for i in range(2)]
            Bw = S.bufs(2, "adaw")
            Bc32, Bcact, Bab = S.buf(), S.buf(), S.buf()
            S.dma("sp", c32[:], self.I["cT"][:, :, :], [], [Bc32])
            S.dma("sp", adab[:], self.I["adaB"][:, :, :], [], [Bab])
            S.op("act", [Bc32], [Bcact], lambda e: e.activation(out=cact[:], in_=c32[:], func=AF.Silu))
            it = 0
            for l in range(2):
                for blk in range(cfg.NB):
                    w, bw = wsl[it % 2], Bw[it % 2]
                    S.dma("pool", w[:], self.I["adaW"][l, blk], [], [bw])
                    for oc in range(4):
                        j = blk * 4 + oc
                        pi = (it * 4 + oc) % 2
                        ps, pb = self.PS[pi], self.PSB[pi]
                        for k in range(KC):
                            S.op("pe", [bw, Bcact], [pb],
                                 lambda e: e.matmul(ps[:, 0:2], w[:, k, oc * 128:(oc + 1) * 128], cact[:, k, :],
                                                    start=(k == 0), stop=(k == KC - 1)), inc=(k == KC - 1))
                        S.op("dve", [pb, Bab], [self.B_mod],
                             lambda e: e.tensor_scalar(out=self.modT[:, l, j, :], in0=ps[:, 0:2],
                                                       scalar1=adab[:, l, j:j + 1], scalar2=None, op0=ALU.add))
                    it += 1
            for l in range(2):
                for sub in range(2):
                    for s in range(2):
                        sc = self.modT[:, l, (3 * sub + 1) * KC:(3 * sub + 2) * KC, s]
                        S.op("dve", [self.B_mod, self.B_const], [self.B_mod],
                             lambda e: e.scalar_tensor_tensor(out=self.acoef[:, l, sub, s, :], in0=sc, scalar=1.0,
                                                              in1=self.normW[:, l, sub, :], op0=ALU.add, op1=ALU.mult))

    def coef(self, l, sub, s, k):
        KC = self.cfg.KC
        A = self.acoef[:, l, sub, s, k:k + 1]
        Bsh = self.modT[:, l, 3 * sub * KC + k, s:s + 1]
        G = self.modT[:, l, (3 * sub + 2) * KC + k, s:s + 1]
        return A, Bsh, G

    def norm_mod(self, T, xt, bx, n, out, bout, l, sub, s, col0=0):
        cfg, S = self.cfg, self.S
        KC = cfg.KC
        import os
        stg = int(os.environ.get("DBG_STAGE", "9"))
        pss, bss = self.PS[6], self.PSB[6]
        if stg < 2: return
        for k in range(KC):
            sq, bsq = T["sq"][k % len(T["sq"])], T["Bsq"][k % len(T["sq"])]
            S.op("act", [bx], [bsq], lambda e: e.activation(out=sq[:, :n], in_=xt[:, k, col0:col0 + n], func=AF.Square))
            S.op("pe", [bsq, self.B_const], [bss],
                 lambda e: e.matmul(pss[:, :n], self.ones[:], sq[:, :n], start=(k == 0), stop=(k == KC - 1)))
        rstd, br = T["rstd"], T["Brstd"]
        if stg < 3: return
        self.rstd_from(pss, bss, n, rstd, br, 1.0 / cfg.D)
        if stg < 4: return
        for k in range(KC):
            A, Bsh, _ = self.coef(l, sub, s, k)
            tmp, bt = T["tmp"][k % len(T["tmp"])], T["Btmp"][k % len(T["tmp"])]
            S.op("dve", [bx, br, self.B_mod], [bt],
                 lambda e: e.scalar_tensor_tensor(out=tmp[:, :n], in0=xt[:, k, col0:col0 + n], scalar=A,
                                                  in1=rstd[:, :n], op0=ALU.mult, op1=ALU.mult))
            if stg < 5: continue
            S.op("act", [bt, self.B_mod], [bout],
                 lambda e: e.activation(out=out[:, k, :n], in_=tmp[:, :n], func=AF.Identity, bias=Bsh, scale=1.0))

    def rstd_from(self, ps, bps, n, rstd, br, inv_n):
        S = self.S
        S.op("act", [bps, self.B_const], [br],
             lambda e: e.activation(out=rstd[:, :n], in_=ps[:, :n], func=AF.Sqrt, bias=self.epsc[:, 0:1], scale=inv_n))
        S.op("dve", [br], [br], lambda e: e.reciprocal(out=rstd[:, :n], in_=rstd[:, :n]))

    def norm_tmps(self, es, tag, ntmp=2, nsq=2):
        S = self.S
        sb = lambda n, s, d: self.sb(es, n, s, d)
        return dict(sq=[sb(f"{tag}sq{i}", [128, 512], BF16) for i in range(nsq)], Bsq=S.bufs(nsq),
                    tmp=[sb(f"{tag}tmp{i}", [128, 512], F32) for i in range(ntmp)], Btmp=S.bufs(ntmp),
                    rstd=sb(f"{tag}rstd", [128, 512], F32), Brstd=S.buf())

    def regions(self):
        cfg = self.cfg
        return [(0, cfg.SEQ, 0), (cfg.SEQ + 2, cfg.CTX, 1)]

    def phase_norm1(self, l, src, bsrc):
        cfg, S = self.cfg, self.S
        KC = cfg.KC
        with ExitStack() as es:
            sb = lambda n, s, d: self.sb(es, n, s, d)
            xts = [sb(f"n1x{i}", [128, KC, 512], F32) for i in range(2)]
            Bx = S.bufs(2)
            hts = [sb(f"n1h{i}", [128, KC, 512], BF16) for i in range(2)]
            Bh = S.bufs(2)
            T = self.norm_tmps(es, "n1", ntmp=8, nsq=4)
            it = 0
            for (base, L, s) in self.regions():
                c0 = base
                while c0 < base + L + 2:
                    n = min(512, base + L + 2 - c0)
                    xt, bx, ht, bh = xts[it % 2], Bx[it % 2], hts[it % 2], Bh[it % 2]
                    S.dma("sp", xt[:, :, :n], src[:, :, c0:c0 + n], [bsrc] if bsrc else [], [bx])
                    self.norm_mod(T, xt, bx, n, ht, bh, l, 0, s)
                    import os
                    if os.environ.get("DBG_NOSTORE") != "1":
                        S.dma("sp", self.hxT[:, :, c0:c0 + n], ht[:, :, :n], [bh], [self.B_hxT])
                    c0 += n
                    it += 1

    def phase_tokens(self, l):
        cfg, S = self.cfg, self.S
        KC, FC = cfg.KC, cfg.FC
        ZC = 2 * KC if l == 0 else KC
        wkind = "retWout" if l == 0 else "attWout"
        wout = self.WB[wkind]
        last = (l == 1)
        regions = self.regions() if l == 0 else [(0, cfg.NQ - 2, 0)]
        with ExitStack() as es:
            sb = lambda n, s, d: self.sb(es, n, s, d)
            UW = max(ZC, FC) * 512
            U = sb("tkU", [128, UW], BF16)
            zt = U[:, 0:ZC * 512].rearrange("p (c n) -> p c n", n=512)
            act = U[:, 0:FC * 512].rearrange("p (c n) -> p c n", n=512)
            BU = S.buf("U")
            xt = sb("tkx", [128, KC, 512], F32)
            Bx = S.buf("xt")
            h2 = sb("tkh2", [128, KC, 512], BF16)
            Bh2 = S.buf("h2")
            wo = [sb(f"tkwo{i}", [128, ZC, 128], BF16) for i in range(3)]
            Bwo = S.bufs(3)
            wu = [sb(f"tkwu{i}", [128, KC, 256], BF16) for i in range(3)]
            Bwu = S.bufs(3)
            wd = [sb(f"tkwd{i}", [128, FC, 128], BF16) for i in range(2)]
            Bwd = S.bufs(2)
            T = self.norm_tmps(es, "tk")
            ca = [sb(f"tkca{i}", [128, 512], F32) for i in range(2)]
            cb = [sb(f"tkcb{i}", [128, 512], F32) for i in range(2)]
            sa = [sb("tksa0", [128, 512], F32)] * 2
            Bca, Bcb, Bsa = S.bufs(2), S.bufs(2), [S.buf()] * 2
            if last:
                ot = sb("tkout", [128, KC, 512], F32)
                Bot = S.buf("ot")
            iwo = iwu = iwd = 0
            for (base, L, s) in regions:
                w0 = 0
                nwin = -(-L // 510)
                widths = [L // nwin + (1 if i < L % nwin else 0) for i in range(nwin)]
                for NV in widths:
                    N = NV + 2
                    c0 = base + w0
                    first_win = (base == 0 and w0 == 0)
                    S.dma("sp", zt[:, :, :N], self.zT[:, 0:ZC, c0:c0 + N], [self.B_zT], [BU])
                    xsrc, bxs = (self.I["xT"], []) if l == 0 else (self.xwT, [self.B_xwT])
                    S.dma("sp", xt[:, :, :N], xsrc[:, :, c0:c0 + N], bxs, [Bx])
                    padcols = ([0] if w0 == 0 else []) + ([N - 1] if w0 + N == L + 2 else [])
                    edge = [(pc, 2 if pc == 0 else 3) for pc in padcols]
                    for pc in (padcols if l == 0 else []):
                        S.op("dve", [], [BU], lambda e: e.memset(zt[:, :, pc:pc + 1], 0.0))
                        S.op("dve", [], [Bx], lambda e: e.memset(xt[:, :, pc:pc + 1], 0.0))
                    for f in range(KC):
                        w, bw = wo[iwo % 3], Bwo[iwo % 3]
                        iwo += 1
                        cvb = self.conv_buf[(wkind, f)]
                        if first_win:
                            S.dma("pool", w[:], self.I[wkind][f], [], [bw])
                            S.dma("sp", wout[f], w[:], [bw], [cvb])
                        else:
                            S.dma("pool", w[:], wout[f], [cvb], [bw])
                        ps, pb = self.PS[f % 2], self.PSB[f % 2]
                        for k in range(ZC):
                            S.op("pe", [bw, BU], [pb],
                                 lambda e: e.matmul(ps[:, :N], w[:, k, :], zt[:, k, :N], start=(k == 0), stop=(k == ZC - 1)),
                                 inc=(k == ZC - 1))
                        _, _, G = self.coef(l, 0, s, f)
                        S.op("dve", [pb, Bx, self.B_mod], [Bx],
                             lambda e: e.scalar_tensor_tensor(out=xt[:, f, :N], in0=ps[:, :N], scalar=G,
                                                              in1=xt[:, f, :N], op0=ALU.mult, op1=ALU.add))
                    self.norm_mod(T, xt, Bx, N, h2, Bh2, l, 1, s)
                    for pc, mi in edge:
                        if l == 0:
                            S.op("dve", [], [Bh2], lambda e: e.memset(h2[:, :, pc:pc + 1], 0.0))
                        else:
                            for k in range(KC):
                                S.op("act", [Bh2, self.B_const], [Bh2],
                                     lambda e: e.activation(out=h2[:, k, pc:pc + 1], in_=h2[:, k, pc:pc + 1], func=AF.Copy,
                                                            scale=self.rsel[:, mi:mi + 1]))
                    for j in range(FC):
                        w, bw = wu[iwu % 3], Bwu[iwu % 3]
                        iwu += 1
                        cvb = self.conv_buf[("ffnUp", l, j)]
                        if first_win:
                            S.dma("pool", w[:], self.I["ffnUp"][l, j], [], [bw])
                            S.dma("sp", self.WB["ffnUp"][l, j], w[:], [bw], [cvb])
                        else:
                            S.dma("pool", w[:], self.WB["ffnUp"][l, j], [cvb], [bw])
                        pa, pab = self.PS[2 + (j % 2) * 2], self.PSB[2 + (j % 2) * 2]
                        pbb, pbbb = self.PS[3 + (j % 2) * 2], self.PSB[3 + (j % 2) * 2]
                        for k in range(KC):
                            S.op("pe", [bw, Bh2], [pab],
                                 lambda e: e.matmul(pa[:, :N], w[:, k, 0:128], h2[:, k, :N], start=(k == 0), stop=(k == KC - 1)),
                                 inc=(k == KC - 1))
                        for k in range(KC):
                            S.op("pe", [bw, Bh2], [pbbb],
                                 lambda e: e.matmul(pbb[:, :N], w[:, k, 128:256], h2[:, k, :N], start=(k == 0), stop=(k == KC - 1)),
                                 inc=(k == KC - 1))
                        for (pp, ppb, cc, ccb, ab) in ((pa, pab, ca[j % 2], Bca[j % 2], 0), (pbb, pbbb, cb[j % 2], Bcb[j % 2], 1)):
                            cw = lambda t: self.convW[:, l, j, ab, t:t + 1]
                            S.op("act", [ppb, self.B_const], [ccb],
                                 lambda e: e.activation(out=cc[:, :NV], in_=pp[:, 1:N - 1], func=AF.Identity,
                                                        bias=self.convB[:, l, j, ab:ab + 1], scale=cw(1)))
                            S.op("dve", [ppb, ccb, self.B_const], [ccb],
                                 lambda e: e.scalar_tensor_tensor(out=cc[:, :NV], in0=pp[:, 0:N - 2], scalar=cw(0),
                                                                  in1=cc[:, :NV], op0=ALU.mult, op1=ALU.add))
                            S.op("dve", [ppb, ccb, self.B_const], [ccb],
                                 lambda e: e.scalar_tensor_tensor(out=cc[:, :NV], in0=pp[:, 2:N], scalar=cw(2),
                                                                  in1=cc[:, :NV], op0=ALU.mult, op1=ALU.add))
                        S.op("act", [Bca[j % 2]], [Bsa[j % 2]],
                             lambda e: e.activation(out=sa[j % 2][:, :NV], in_=ca[j % 2][:, :NV], func=AF.Silu))
                        S.op("dve", [Bsa[j % 2], Bcb[j % 2]], [BU],
                             lambda e: e.tensor_tensor(out=act[:, j, :NV], in0=sa[j % 2][:, :NV], in1=cb[j % 2][:, :NV], op=ALU.mult))
                    for f in range(KC):
                        w, bw = wd[iwd % 2], Bwd[iwd % 2]
                        iwd += 1
                        cvb = self.conv_buf[("ffnDown", l, f)]
                        if first_win:
                            S.dma("pool", w[:], self.I["ffnDown"][l, f], [], [bw])
                            S.dma("sp", self.WB["ffnDown"][l, f], w[:], [bw], [cvb])
                        else:
                            S.dma("pool", w[:], self.WB["ffnDown"][l, f], [cvb], [bw])
                        ps, pb = self.PS[f % 2], self.PSB[f % 2]
                        for j in range(FC):
                            S.op("pe", [bw, BU], [pb],
                                 lambda e: e.matmul(ps[:, :NV], w[:, j, :], act[:, j, :NV], start=(j == 0), stop=(j == FC - 1)),
                                 inc=(j == FC - 1))
                        _, _, G = self.coef(l, 1, s, f)
                        S.op("dve", [pb, Bx, self.B_mod], [Bx],
                             lambda e: e.scalar_tensor_tensor(out=xt[:, f, 1:N - 1], in0=ps[:, :NV], scalar=G,
                                                              in1=xt[:, f, 1:N - 1], op0=ALU.mult, op1=ALU.add))
                    if not last:
                        lo = 0 if w0 == 0 else 1
                        hi = N if w0 + N == L + 2 else N - 1
                        S.dma("sp", self.x1T[:, :, c0 + lo:c0 + hi], xt[:, :, lo:hi], [Bx], [self.B_x1T])
                        if self.debug_x1:
                            S.dma("sp", self.dbgT[:, :, c0 + 1:c0 + 1 + NV], xt[:, :, 1:N - 1], [Bx], [self.B_out])
                    else:
                        pss, bss = self.PS[6], self.PSB[6]
                        for k in range(KC):
                            sq, bsq = T["sq"][k % 2], T["Bsq"][k % 2]
                            S.op("act", [Bx], [bsq], lambda e: e.activation(out=sq[:, :NV], in_=xt[:, k, 1:N - 1], func=AF.Square))
                            S.op("pe", [bsq, self.B_const], [bss],
                                 lambda e: e.matmul(pss[:, :NV], self.ones[:], sq[:, :NV], start=(k == 0), stop=(k == KC - 1)))
                        rstd, br = T["rstd"], T["Brstd"]
                        self.rstd_from(pss, bss, NV, rstd, br, 1.0 / cfg.D)
                        for k in range(KC):
                            S.op("dve", [Bx, br, self.B_const], [Bot],
                                 lambda e: e.scalar_tensor_tensor(out=ot[:, k, :NV], in0=xt[:, k, 1:N - 1], scalar=self.fnW[:, k:k + 1],
                                                                  in1=rstd[:, :NV], op0=ALU.mult, op1=ALU.mult))
                        S.dma("sp", self.outT[:, :, w0:w0 + NV], ot[:, :, :NV], [Bot], [self.B_out])
                    w0 += NV

    def phase_retention(self):
        cfg, S = self.cfg, self.S
        KC, RH, NCH, NT, CTX, SEQ = cfg.KC, cfg.RH, cfg.NCH, cfg.NT, cfg.CTX, cfg.SEQ
        CC = CTX // 128
        with ExitStack() as es:
            sb = lambda n, s, d: self.sb(es, n, s, d)
            ld = sb("r_ld", [128, 2 * RH], F32)
            lg = sb("r_lg", [128, 2 * RH], F32)
            dtab = sb("r_dtab", [128, 5, 2 * RH], F32)
            idxc = sb("r_idx", [128, 5], F32)
            maskc = sb("r_maskc", [128, 4, 128], F32)
            Bk = S.buf("rconst")
            S.dma("sp", ld[:], self.I["retLD"][0:1, :].partition_broadcast(128), [], [Bk])
            S.dma("sp", idxc[:], self.I["idxc"][:, :], [], [Bk])
            S.dma("sp", maskc[:], self.I["maskc"][:, :, :], [], [Bk])
            S.op("act", [Bk], [Bk], lambda e: e.activation(out=lg[:], in_=ld[:], func=AF.Exp))
            S.op("dve", [Bk], [Bk], lambda e: e.tensor_scalar(out=lg[:], in0=lg[:], scalar1=-1.0, scalar2=None, op0=ALU.mult))
            for t in range(5):
                S.op("dve", [Bk], [Bk], lambda e: e.tensor_scalar(out=dtab[:, t, :], in0=lg[:], scalar1=idxc[:, t:t + 1],
                                                                  scalar2=None, op0=ALU.mult))
            S.op("act", [Bk], [Bk], lambda e: e.activation(out=dtab[:], in_=dtab[:], func=AF.Exp))

            HXW = 256
            hxs = [sb(f"r_hx{i}", [128, KC, HXW], BF16) for i in range(2)]
            Bhx = S.bufs(2)
            ws = [sb(f"r_w{i}", [128, KC, 512], BF16) for i in range(3)]
            Bw = S.bufs(3)
            ropes = [sb("r_rope0", [128, 2, 512], F32)]
            Brope = S.bufs(1)
            qT = sb("r_qT", [128, 2, NT], BF16)
            kT = sb("r_kT", [128, 2, NT], BF16)
            v = sb("r_v", [128, NCH, 512], BF16)
            sgw = sb("r_sgw", [128, NCH, 512], BF16)
            yb = sb("r_yb", [128, NCH, 512], BF16)
            BqT, BkT = S.bufs(NCH, "qT"), S.bufs(NCH, "kT")
            Bv, Bsgw, Byb = S.bufs(NCH, "v"), S.bufs(NCH, "sgw"), S.bufs(NCH, "yb")
            gnw = sb("r_gnw", [128, 512], F32)
            Bgnw = S.buf()
            mask = sb("r_mask", [128, 128], F32)
            mtmp = sb("r_mtmp", [128, 128], F32)
            Bmask = S.buf()
            raw = [sb("r_raw0", [128, 512], BF16)] * 2
            Braw = [S.buf()] * 2
            t1 = [sb("r_t10", [128, 512], F32)] * 2
            Bt1 = [S.buf()] * 2
            t2 = [sb("r_t20", [128, 512], F32)] * 2
            Bt2 = [S.buf()] * 2
            kd = {d: [sb(f"r_kd{d}{i}", [128, 256], BF16) for i in range(2)] for d in (0, 1)}
            Bkd = {d: S.bufs(2) for d in (0, 1)}
            Bptk = {0: S.buf("ptk0"), 1: S.buf("ptk1")}
            sT = [sb(f"r_sT{i}", [128, 128], BF16) for i in range(2)]
            BsT = S.bufs(2)
            st32 = {d: sb(f"r_st32{d}", [128, 2, 512], F32) for d in (0, 1)}
            st16 = {d: [sb(f"r_st16{d}{i}", [128, 2, 512], BF16) for i in range(3)] for d in (0, 1)}
            Bst32 = {d: S.buf() for d in (0, 1)}
            Bst16 = {d: S.bufs(3) for d in (0, 1)}
            yy = [sb(f"r_y{i}", [128, 512], F32) for i in range(4)]
            By = S.bufs(4)
            stat = [sb(f"r_stat{i}", [128, 8], F32) for i in range(4)]
            Bstat = S.bufs(4)
            zz = [sb(f"r_z{i}", [128, 512], BF16) for i in range(2)]
            Bz = S.bufs(2)
            NZ = 4
            zTt = [sb(f"r_zT{i}", [128, 4, 128], BF16) for i in range(NZ)]
            BzT = S.bufs(NZ)
            Bgate = S.buf("gate")

            tiles = []
            s0 = 0
            while s0 < NT:
                lim = CTX if s0 < CTX else NT
                n = min(HXW, lim - s0)
                tiles.append((s0, n))
                s0 += n
            ihx = iw = irope = iraw = 0
            for h in range(RH):
                dcol = lambda d: d * RH + h
                S.dma("sp", gnw[:], self.I["retGN"][0:1, 512 * h:512 * h + 512].partition_broadcast(128), [], [Bgnw])
                S.op("act", [Bk], [Bmask], lambda e: e.activation(out=mask[:], in_=maskc[:, 0, :], func=AF.Exp,
                                                                  scale=lg[:, dcol(0):dcol(0) + 1]))
                S.op("dve", [Bmask, Bk], [Bmask], lambda e: e.tensor_tensor(out=mask[:], in0=mask[:], in1=maskc[:, 1, :], op=ALU.mult))
                S.op("act", [Bk, Bmask], [Bmask], lambda e: e.activation(out=mtmp[:], in_=maskc[:, 2, :], func=AF.Exp,
                                                                         scale=lg[:, dcol(1):dcol(1) + 1]))
                S.op("dve", [Bmask, Bk], [Bmask], lambda e: e.tensor_tensor(out=mtmp[:], in0=mtmp[:], in1=maskc[:, 3, :], op=ALU.mult))
                S.op("dve", [Bmask], [Bmask], lambda e: e.tensor_tensor(out=mask[:], in0=mask[:], in1=mtmp[:], op=ALU.add))
                wb = []
                for j in range(3):
                    w, bw = ws[iw % 3], Bw[iw % 3]
                    iw += 1
                    S.dma("pool", w[:], self.I["retWin"][h, j], [], [bw])
                    wb.append((w, bw))
                (wqk, bwqk), (wv, bwv), (wg, bwg) = wb
                for (s0, n) in tiles:
                    hx, bhx = hxs[ihx % 2], Bhx[ihx % 2]
                    ihx += 1
                    col = self.seq_col(s0)
                    S.dma("sp", hx[:, :, :n], self.hxT[:, :, col:col + n], [self.B_hxT], [bhx])
                    lat = s0 >= CTX
                    chs = list(range(s0 // 128, (s0 + n) // 128))
                    rp, brp = ropes[0], Brope[0]
                    t0 = s0 - CTX
                    for oc in (0, 2, 1, 3):
                        dst, bdst = (qT, BqT) if oc < 2 else (kT, BkT)
                        j = oc % 2
                        if lat and oc < 2:
                            for a_ in range(2):
                                S.dma("sp", rp[:, a_, :n], self.I["ropeR"][a_, :, j, t0:t0 + n], [], [brp])
                        scale = 1.0 if oc < 2 else 1.0 / 16.0
                        ps, pb = self.PS[oc % 2], self.PSB[oc % 2]
                        for k in range(KC):
                            S.op("pe", [bwqk, bhx], [pb],
                                 lambda e: e.matmul(ps[:, :n], wqk[:, k, oc * 128:(oc + 1) * 128], hx[:, k, :n],
                                                    start=(k == 0), stop=(k == KC - 1)), inc=(k == KC - 1))
                        wr = [bdst[c] for c in chs]
                        if not lat:
                            S.op("act", [pb], wr, lambda e: e.activation(out=dst[:, j, s0:s0 + n], in_=ps[:, :n], func=AF.Copy, scale=scale))
                        else:
                            rw_, brw = raw[iraw % 2], Braw[iraw % 2]
                            a1, ba1 = t1[iraw % 2], Bt1[iraw % 2]
                            a2, ba2 = t2[iraw % 2], Bt2[iraw % 2]
                            iraw += 1
                            S.op("act", [pb], [brw], lambda e: e.activation(out=rw_[:, :n], in_=ps[:, :n], func=AF.Copy, scale=scale))
                            pr, pbr = self.PS[2], self.PSB[2]
                            S.op("pe", [brw, self.B_const], [pbr], lambda e: e.matmul(pr[:, :n], self.protT[:], rw_[:, :n], start=True, stop=True))
                            S.op("dve", [brw, brp], [ba1], lambda e: e.tensor_tensor(out=a1[:, :n], in0=rw_[:, :n], in1=rp[:, 0, :n], op=ALU.mult))
                            S.op("dve", [pbr, brp], [ba2], lambda e: e.tensor_tensor(out=a2[:, :n], in0=pr[:, :n], in1=rp[:, 1, :n], op=ALU.mult))
                            S.op("dve", [ba1, ba2], wr, lambda e: e.tensor_tensor(out=dst[:, j, s0:s0 + n], in0=a1[:, :n], in1=a2[:, :n], op=ALU.add))
                    for c in chs:
                        o = c * 128 - s0
                        ps, pb = self.PS[3], self.PSB[3]
                        for k in range(KC):
                            S.op("pe", [bwv, bhx], [pb], lambda e: e.matmul(ps[:, :], hx[:, k, o:o + 128], wv[:, k, :],
                                                                            start=(k == 0), stop=(k == KC - 1)), inc=(k == KC - 1))
                        S.op("act", [pb], [Bv[c]], lambda e: e.activation(out=v[:, c, :], in_=ps[:, :], func=AF.Copy))
                        ps, pb = self.PS[4], self.PSB[4]
                        for k in range(KC):
                            S.op("pe", [bwg, bhx], [pb], lambda e: e.matmul(ps[:, :], hx[:, k, o:o + 128], wg[:, k, :],
                                                                            start=(k == 0), stop=(k == KC - 1)), inc=(k == KC - 1))
                        a1, ba1 = t1[c % 2], Bt1[c % 2]
                        S.op("act", [pb], [ba1], lambda e: e.activation(out=a1[:, :], in_=ps[:, :], func=AF.Silu))
                        S.op("dve", [ba1, Bgnw], [Bsgw[c]], lambda e: e.tensor_tensor(out=sgw[:, c, :], in0=a1[:, :], in1=gnw[:, :], op=ALU.mult))

                fwd = list(range(NCH))
                bwd = list(range(CC - 1, -1, -1)) + list(range(NCH - 1, CC - 1, -1))
                orders = {1: bwd, 0: fwd}
                pos = {d: {c: i for i, c in enumerate(orders[d])} for d in (0, 1)}
                for d in (0, 1):
                    S.op("dve", [], [Bst32[d]], lambda e: e.memset(st32[d][:], 0.0))
                    S.op("dve", [], [Bst16[d][0]], lambda e: e.memset(st16[d][0][:], 0.0))
                qd = {d: dtab[:, 0 if d == 0 else 1, dcol(d):dcol(d) + 1] for d in (0, 1)}
                kdc = {d: dtab[:, 2 if d == 0 else 3, dcol(d):dcol(d) + 1] for d in (0, 1)}
                cdec = {d: dtab[:, 4, dcol(d):dcol(d) + 1] for d in (0, 1)}
                for i in range(NCH + 1):
                    if i < NCH:
                        for d in (1, 0):
                            c = orders[d][i]
                            cs = slice(c * 128, (c + 1) * 128)
                            for j in range(2):
                                S.op("pe", [BkT[c], self.B_const], [Bptk[d]],
                                     lambda e: e.transpose(self.PT[:, d * 256 + j * 128:d * 256 + (j + 1) * 128], kT[:, j, cs], self.ident[:]),
                                     inc=(j == 1))
                        for d in (1, 0):
                            kd_, bkd = kd[d][i % 2], Bkd[d][i % 2]
                            S.op("act", [Bptk[d], Bk], [bkd], lambda e: e.activation(out=kd_[:], in_=self.PT[:, d * 256:(d + 1) * 256],
                                                                                     func=AF.Copy, scale=kdc[d]))
                        for d in (1, 0):
                            c = orders[d][i]
                            kd_, bkd = kd[d][i % 2], Bkd[d][i % 2]
                            for j in range(2):
                                pU, pUb = self.PS[(1 - d) * 2 + j], self.PSB[(1 - d) * 2 + j]
                                S.op("pe", [bkd, Bv[c]], [pUb], lambda e: e.matmul(pU[:, :], kd_[:, j * 128:(j + 1) * 128], v[:, c, :], start=True, stop=True))
                        for d in (1, 0):
                            for j in range(2):
                                pU, pUb = self.PS[(1 - d) * 2 + j], self.PSB[(1 - d) * 2 + j]
                                S.op("dve", [pUb, Bst32[d], Bk], [Bst32[d]],
                                     lambda e: e.scalar_tensor_tensor(out=st32[d][:, j, :], in0=st32[d][:, j, :], scalar=cdec[d], in1=pU[:, :],
                                                                      op0=ALU.mult, op1=ALU.add))
                        for d in (1, 0):
                            nx = (i + 1) % 3
                            if d == 1:
                                S.op("act", [Bst32[d]], [Bst16[d][nx]], lambda e: e.activation(out=st16[d][nx][:], in_=st32[d][:], func=AF.Copy))
                            else:
                                S.op("pool", [Bst32[d]], [Bst16[d][nx]], lambda e: e.tensor_copy(out=st16[d][nx][:], in_=st32[d][:]))
                    if i >= 1:
                        for d in (1, 0):
                            c = orders[d][i - 1]
                            cs = slice(c * 128, (c + 1) * 128)
                            cur = (i - 1) % 3
                            pB, pBb = self.PS[4 + (1 - d)], self.PSB[4 + (1 - d)]
                            for j in range(2):
                                S.op("pe", [BqT[c], Bst16[d][cur]], [pBb], lambda e: e.matmul(pB[:, :], qT[:, j, cs], st16[d][cur][:, j, :],
                                                                                              start=(j == 0), stop=(j == 1)), inc=(j == 1))
                            first = (pos[d][c] < pos[1 - d][c]) or (pos[d][c] == pos[1 - d][c] and d == 1)
                            if first:
                                S.op("act", [pBb, Bk], [Byb[c]], lambda e: e.activation(out=yb[:, c, :], in_=pB[:, :], func=AF.Copy, scale=qd[d]))
                            else:
                                S.op("dve", [pBb, Byb[c], Bk], [Byb[c]],
                                     lambda e: e.scalar_tensor_tensor(out=yb[:, c, :], in0=pB[:, :], scalar=qd[d], in1=yb[:, c, :],
                                                                      op0=ALU.mult, op1=ALU.add))

                NY = len(yy)

                def s0(c):
                    cs = slice(c * 128, (c + 1) * 128)
                    pS, pSb = self.PS[4 + c % 2], self.PSB[4 + c % 2]
                    for j in range(2):
                        S.op("pe", [BkT[c], BqT[c]], [pSb], lambda e: e.matmul(pS[:, 0:128], kT[:, j, cs], qT[:, j, cs],
                                                                               start=(j == 0), stop=(j == 1)), inc=(j == 1))

                def s1(c):
                    pS, pSb = self.PS[4 + c % 2], self.PSB[4 + c % 2]
                    S.op("dve", [pSb, Bmask], [BsT[c % 2]], lambda e: e.tensor_tensor(out=sT[c % 2][:], in0=pS[:, 0:128], in1=mask[:], op=ALU.mult))

                def s2(c):
                    pA, pAb = (self.PS[6], self.PSB[6]) if c % 2 == 0 else (self.PS[3], self.PSB[3])
                    S.op("pe", [BsT[c % 2], Bv[c]], [pAb], lambda e: e.matmul(pA[:, :], sT[c % 2][:], v[:, c, :], start=True, stop=True))

                def s3(c):
                    pA, pAb = (self.PS[6], self.PSB[6]) if c % 2 == 0 else (self.PS[3], self.PSB[3])
                    y, by = yy[c % NY], By[c % NY]
                    sst, bsst = stat[c % NY], Bstat[c % NY]
                    S.op("dve", [pAb, Byb[c]], [by], lambda e: e.tensor_tensor(out=y[:], in0=pA[:, :], in1=yb[:, c, :], op=ALU.add))
                    S.op("dve", [by], [bsst], lambda e: e.bn_stats(out=sst[:, 2:8], in_=y[:]))
                    S.op("dve", [bsst], [bsst], lambda e: e.bn_aggr(out=sst[:, 0:2], in_=sst[:, 2:8]))

                def s4(c):
                    sst, bsst = stat[c % NY], Bstat[c % NY]
                    S.op("act", [bsst, self.B_const], [bsst], lambda e: e.activation(out=sst[:, 2:3], in_=sst[:, 1:2], func=AF.Sqrt,
                                                                                     bias=self.epsc[:, 0:1], scale=1.0))

                def s5(c):
                    sst, bsst = stat[c % NY], Bstat[c % NY]
                    S.op("dve", [bsst], [bsst], lambda e: e.reciprocal(out=sst[:, 2:3], in_=sst[:, 2:3]))
                    S.op("dve", [bsst], [bsst], lambda e: e.scalar_tensor_tensor(out=sst[:, 3:4], in0=sst[:, 0:1], scalar=-1.0,
                                                                                 in1=sst[:, 2:3], op0=ALU.mult, op1=ALU.mult))

                def s6(c):
                    y, by = yy[c % NY], By[c % NY]
                    sst, bsst = stat[c % NY], Bstat[c % NY]
                    S.op("act", [by, bsst], [by], lambda e: e.activation(out=y[:], in_=y[:], func=AF.Identity,
                                                                         bias=sst[:, 3:4], scale=sst[:, 2:3]))

                def s7(c):
                    y, by = yy[c % NY], By[c % NY]
                    S.op("dve", [by, Bsgw[c]], [Bz[c % 2]], lambda e: e.tensor_tensor(out=zz[c % 2][:], in0=y[:], in1=sgw[:, c, :], op=ALU.mult))

                def s8(c):
                    z, bz = zz[c % 2], Bz[c % 2]
                    for j in range(4):
                        S.op("pe", [bz, self.B_const], [self.PTB[1]],
                             lambda e: e.transpose(self.PT[:, 512 + j * 128:512 + (j + 1) * 128], z[:, j * 128:(j + 1) * 128], self.ident[:]),
                             inc=(j == 3))

                def s9(c):
                    zt_, bzt = zTt[c % NZ], BzT[c % NZ]
                    S.op("act", [self.PTB[1]], [bzt, Bgate], lambda e: e.activation(out=zt_[:].rearrange("p c n -> p (c n)"), in_=self.PT[:, 512:1024], func=AF.Copy))
                    col = self.seq_col(c * 128)
                    S.dma("sp", self.zT[:, 4 * h:4 * h + 4, col:col + 128], zt_[:], [bzt], [self.B_zT])

                stages3 = [s0, s1, s2, s3, s4, s5, s6, s7, s8, s9]
                for it in range(NCH + len(stages3) - 1):
                    for k in reversed(range(len(stages3))):
                        if 0 <= it - k < NCH:
                            stages3[k](it - k)

    def phase_window(self):
        cfg, S = self.cfg, self.S
        KC, NQ, H = cfg.KC, cfg.NQ, cfg.HALF
        with ExitStack() as es:
            sb = lambda n, s, d: self.sb(es, n, s, d)
            xa = sb("wxa", [128, KC, 512], F32)
            xb = sb("wxb", [128, KC, 512], F32)
            ht = sb("wh", [128, KC, 512], BF16)
            Ba, Bb, Bh = S.buf(), S.buf(), S.buf()
            T = self.norm_tmps(es, "w", ntmp=8, nsq=4)
            j0 = 0
            while j0 < NQ:
                n = min(512, NQ - j0)
                S.dma("sp", xa[:, :, :n], self.x1T[:, :, j0:j0 + n], [self.B_x1T], [Ba])
                S.dma("sp", xb[:, :, :n], self.x1T[:, :, H + j0:H + j0 + n], [self.B_x1T], [Bb])
                S.op("dve", [Bb, self.B_const], [Bb],
                     lambda e: e.tensor_scalar(out=xb[:, :, :n], in0=xb[:, :, :n], scalar1=self.rsel[:, 1:2], scalar2=None, op0=ALU.mult))
                S.op("dve", [Ba, Bb, self.B_const], [Ba],
                     lambda e: e.scalar_tensor_tensor(out=xa[:, :, :n], in0=xa[:, :, :n], scalar=self.rsel[:, 0:1], in1=xb[:, :, :n],
                                                      op0=ALU.mult, op1=ALU.add))
                S.dma("sp", self.xwT[:, :, j0:j0 + n], xa[:, :, :n], [Ba], [self.B_xwT])
                self.norm_mod(T, xa, Ba, n, ht, Bh, 1, 0, 0)
                S.dma("sp", self.hxwT[:, :, j0:j0 + n], ht[:, :, :n], [Bh], [self.B_hxwT])
                j0 += n

    def phase_attention(self):
        cfg, S = self.cfg, self.S
        KC, AKV, NCH, NT, CTX, SEQ, KVW = cfg.KC, cfg.AKV, cfg.NCH, cfg.NT, cfg.CTX, cfg.SEQ, cfg.KVW
        scale = 1.0 / math.sqrt(128.0)
        with ExitStack() as es:
            sb = lambda n, s, d: self.sb(es, n, s, d)
            kT = sb("a_kT", [128, AKV, NT], BF16)
            vv = sb("a_v", [128, NCH, KVW], BF16)
            BkT = S.buf("a_kT")
            Bv = S.bufs(NCH, "a_v")
            qT = sb("a_qT", [128, 4, cfg.NQ], BF16)
            BqT = S.buf("a_qT")
            hxs = [sb(f"a_hx{i}", [128, KC, 512], BF16) for i in range(2)]
            Bhx = S.bufs(2)
            wk = sb("a_wk", [128, KC, KVW], BF16)
            wv = sb("a_wv", [128, KC, KVW], BF16)
            wq = [sb(f"a_wq{i}", [128, KC, 512], BF16) for i in range(2)]
            Bwk, Bwv = S.buf(), S.buf()
            Bwq = S.bufs(2)
            ropes = [sb(f"a_rope{i}", [128, 2, 512], F32) for i in range(2)]
            Brope = S.bufs(2)
            qn = sb("a_qn", [128, 1], F32)
            kn = sb("a_kn", [128, 1], F32)
            smb = sb("a_smb", [128, 1], F32)
            Bn = S.buf()
            S.dma("sp", qn[:], self.I["attQN"][:, :], [], [Bn])
            S.dma("sp", kn[:], self.I["attKN"][:, :], [], [Bn])
            S.op("dve", [], [Bn], lambda e: e.memset(smb[:], SM_BIAS))
            sq = [sb(f"a_sq{i}", [128, 512], BF16) for i in range(2)]
            Bsq = S.bufs(2)
            rstd = [sb(f"a_rstd{i}", [128, 512], F32) for i in range(2)]
            Brstd = S.bufs(2)
            nrm = [sb(f"a_nrm{i}", [128, 512], BF16) for i in range(2)]
            Bnrm = S.bufs(2)
            t1 = [sb(f"a_t1{i}", [128, 512], F32) for i in range(2)]
            Bt1 = S.bufs(2)
            t2 = [sb(f"a_t2{i}", [128, 512], F32) for i in range(2)]
            Bt2 = S.bufs(2)
            pT = [sb(f"a_pT{i}", [128, 512], BF16) for i in range(4)]
            BpT = S.bufs(4)
            rd = [sb(f"a_rd{i}", [128, 512], F32) for i in range(2)]
            Brd = S.bufs(2)
            oo = [sb(f"a_o{i}", [128, 512], BF16) for i in range(2)]
            Bo = S.bufs(2)
            S.dma("pool", wk[:], self.I["attWk"][:, :, :], [], [Bwk])
            S.dma("pool", wv[:], self.I["attWv"][:, :, :], [], [Bwv])
            cnt = dict(hx=0, rope=0, n=0)

            def load_hx(col, n, src=None, bsrc=None):
                src = self.hxT if src is None else src
                bsrc = self.B_hxT if bsrc is None else bsrc
                hx, bhx = hxs[cnt["hx"] % 2], Bhx[cnt["hx"] % 2]
                cnt["hx"] += 1
                S.dma("sp", hx[:, :, :n], src[:, :, col:col + n], [bsrc], [bhx])
                return hx, bhx

            def load_rope(t0, n, tab="ropeA"):
                rp, brp = ropes[cnt["rope"] % 2], Brope[cnt["rope"] % 2]
                cnt["rope"] += 1
                for a_ in range(2):
                    S.dma("sp", rp[:, a_, :n], self.I[tab][a_, :, t0:t0 + n], [], [brp])
                return rp, brp

            def head_norm_rope(ps, pb, n, nw, rope, dst_ap, bdst):
                i = cnt["n"] % 2
                cnt["n"] += 1
                S.op("act", [pb], [Bsq[i]], lambda e: e.activation(out=sq[i][:, :n], in_=ps[:, :n], func=AF.Square))
                p2, p2b = self.PS[2], self.PSB[2]
                S.op("pe", [Bsq[i], self.B_const], [p2b], lambda e: e.matmul(p2[:, :n], self.ones[:], sq[i][:, :n], start=True, stop=True))
                self.rstd_from(p2, p2b, n, rstd[i], Brstd[i], 1.0 / 128.0)
                if rope is None:
                    S.op("dve", [pb, Brstd[i], Bn], [bdst],
                         lambda e: e.scalar_tensor_tensor(out=dst_ap, in0=ps[:, :n], scalar=nw, in1=rstd[i][:, :n], op0=ALU.mult, op1=ALU.mult))
                    return
                rp, brp = rope
                S.op("dve", [pb, Brstd[i], Bn], [Bnrm[i]],
                     lambda e: e.scalar_tensor_tensor(out=nrm[i][:, :n], in0=ps[:, :n], scalar=nw, in1=rstd[i][:, :n], op0=ALU.mult, op1=ALU.mult))
                p3, p3b = self.PS[3], self.PSB[3]
                S.op("pe", [Bnrm[i], self.B_const], [p3b], lambda e: e.matmul(p3[:, :n], self.protT[:], nrm[i][:, :n], start=True, stop=True))
                S.op("dve", [Bnrm[i], brp], [Bt1[i]], lambda e: e.tensor_tensor(out=t1[i][:, :n], in0=nrm[i][:, :n], in1=rp[:, 0, :n], op=ALU.mult))
                S.op("dve", [p3b, brp], [Bt2[i]], lambda e: e.tensor_tensor(out=t2[i][:, :n], in0=p3[:, :n], in1=rp[:, 1, :n], op=ALU.mult))
                S.op("dve", [Bt1[i], Bt2[i]], [bdst], lambda e: e.tensor_tensor(out=dst_ap, in0=t1[i][:, :n], in1=t2[i][:, :n], op=ALU.add))

            tiles = []
            s0 = 0
            while s0 < NT:
                lim = CTX if s0 < CTX else NT
                n = min(512, lim - s0)
                tiles.append((s0, n))
                s0 += n
            for (s0, n) in tiles:
                hx, bhx = load_hx(self.seq_col(s0), n)
                lat = s0 >= CTX
                rope = load_rope(s0 - CTX, n) if lat else None
                for g in range(AKV):
                    ps, pb = self.PS[g % 2], self.PSB[g % 2]
                    for k in range(KC):
                        S.op("pe", [Bwk, bhx], [pb], lambda e: e.matmul(ps[:, :n], wk[:, k, g * 128:(g + 1) * 128], hx[:, k, :n],
                                                                        start=(k == 0), stop=(k == KC - 1)), inc=(k == KC - 1))
                    head_norm_rope(ps, pb, n, kn[:, 0:1], rope, kT[:, g, s0:s0 + n], BkT)
                for c in range(s0 // 128, (s0 + n) // 128):
                    o = c * 128 - s0
                    ps, pb = self.PS[4], self.PSB[4]
                    for k in range(KC):
                        S.op("pe", [Bwv, bhx], [pb], lambda e: e.matmul(ps[:, :KVW], hx[:, k, o:o + 128], wv[:, k, :],
                                                                        start=(k == 0), stop=(k == KC - 1)), inc=(k == KC - 1))
                    S.op("act", [pb], [Bv[c]], lambda e: e.activation(out=vv[:, c, :], in_=ps[:, :KVW], func=AF.Copy))

            NQ = cfg.NQ
            qtiles = [(t0, min(512, NQ - t0)) for t0 in range(0, NQ, 512)]
            io = ip = 0
            for g in range(AKV):
                w, bw = wq[g % 2], Bwq[g % 2]
                S.dma("pool", w[:], self.I["attWq"][g], [], [bw])
                for (t0, n) in qtiles:
                    hx, bhx = load_hx(t0, n, self.hxwT, self.B_hxwT)
                    rope = load_rope(t0, n, "ropeAw")
                    for hq in range(4):
                        ps, pb = self.PS[hq % 2], self.PSB[hq % 2]
                        for k in range(KC):
                            S.op("pe", [bw, bhx], [pb], lambda e: e.matmul(ps[:, :n], w[:, k, hq * 128:(hq + 1) * 128], hx[:, k, :n],
                                                                           start=(k == 0), stop=(k == KC - 1)), inc=(k == KC - 1))
                        head_norm_rope(ps, pb, n, qn[:, 0:1], rope, qT[:, hq, t0:t0 + n], BqT)
                for hq in range(4):
                    head = g * 4 + hq
                    for (t0, n) in qtiles:
                        ob = (5, 6) if io % 2 == 0 else (3, 4)
                        pO, pOb = self.PS[ob[0]], self.PSB[ob[0]]
                        pD, pDb = self.PS[ob[1]], self.PSB[ob[1]]
                        pend = {}

                        def emit_s(c):
                            nonlocal ip
                            pS, pSb = self.PS[c % 2], self.PSB[c % 2]
                            S.op("pe", [BkT, BqT], [pSb], lambda e: e.matmul(pS[:, :n], kT[:, g, c * 128:(c + 1) * 128], qT[:, hq, t0:t0 + n],
                                                                             start=True, stop=True))
                            p_, bp_ = pT[ip % 4], BpT[ip % 4]
                            ip += 1
                            S.op("act", [pSb, Bn], [bp_], lambda e: e.activation(out=p_[:, :n], in_=pS[:, :n], func=AF.Exp,
                                                                                 bias=smb[:, 0:1], scale=scale))
                            pend[c] = (p_, bp_)

                        emit_s(0)
                        for c in range(NCH):
                            if c + 1 < NCH:
                                emit_s(c + 1)
                            p_, bp_ = pend.pop(c)
                            S.op("pe", [bp_, Bv[c]], [pOb], lambda e: e.matmul(pO[:, :n], vv[:, c, g * 128:(g + 1) * 128], p_[:, :n],
                                                                               start=(c == 0), stop=(c == NCH - 1)), inc=(c == NCH - 1))
                            S.op("pe", [bp_, self.B_const], [pDb], lambda e: e.matmul(pD[:, :n], self.ones[:], p_[:, :n],
                                                                                      start=(c == 0), stop=(c == NCH - 1)), inc=True)
                        r_, br_ = rd[io % 2], Brd[io % 2]
                        S.op("dve", [pDb], [br_], lambda e: e.reciprocal(out=r_[:, :n], in_=pD[:, :n]))
                        o_, bo_ = oo[io % 2], Bo[io % 2]
                        io += 1
                        S.op("dve", [pOb, br_], [bo_], lambda e: e.tensor_tensor(out=o_[:, :n], in0=pO[:, :n], in1=r_[:, :n], op=ALU.mult))
                        S.dma("sp", self.zT[:, head, t0:t0 + n], o_[:, :n], [bo_], [self.B_zT])


def build_program(cfg, **kw):
    b = Builder(cfg, **kw)
    b.build()
    return b


def kernel(**inputs):
    cfg = Cfg()
    W = prep_weights(cfg, inputs)
    consts = make_consts(cfg)
    W.update(consts)
    n_cores = 2 * cfg.BATCH
    in_maps = []
    for i in range(n_cores):
        b, r = i // 2, i % 2
        m = dict(W)
        m.update(prep_core(cfg, inputs, b))
        m.update(prep_half(cfg, consts, r))
        in_maps.append(m)
    bld = build_program(cfg)
    res = run_bass_kernel_spmd(bld.nc, in_maps, core_ids=list(range(n_cores)))
    out = np.empty((cfg.BATCH, cfg.SEQ, cfg.D), np.float32)
    for i in range(n_cores):
        b, r = i // 2, i % 2
        oT = res.results[i]["outT"]
        out[b, r * cfg.HALF:(r + 1) * cfg.HALF] = oT.transpose(2, 1, 0).reshape(cfg.HALF, cfg.D)
    return out
```

```python
import math
from contextlib import ExitStack

import numpy as np
import concourse.bass as bass
import concourse.mybir as mybir
from concourse.bass_utils import run_bass_kernel_spmd

F32 = mybir.dt.float32
BF16 = mybir.dt.bfloat16
ALU = mybir.AluOpType
AF = mybir.ActivationFunctionType
EPS = 1e-6
SM_BIAS = -12.0


class Cfg:
    def __init__(self, D=2048, SEQ=2048, CTX=256, BATCH=4):
        self.D, self.SEQ, self.CTX, self.BATCH = D, SEQ, CTX, BATCH
        self.KC = D // 128
        self.RH = D // 256
        self.AH = D // 128
        self.AKV = self.AH // 4
        self.KVW = self.AKV * 128
        self.FF = 256 * ((8 * D // 3 + 255) // 256)
        self.FC = self.FF // 128
        self.TOT = SEQ + CTX + 4
        self.NT = SEQ + CTX
        self.NCH = self.NT // 128
        self.NB = 6 * D // 512
        self.GRID_W = 64
        self.HALF = SEQ // 2
        self.NQ = SEQ // 2 + 2


class Buf:
    __slots__ = ("name", "w", "r")

    def __init__(self, name=""):
        self.name = name
        self.w = {}
        self.r = {}


class Sched:
    def __init__(self, nc, es, n_slots=8):
        self.nc = nc
        self.eng = {}
        for name, h in (("pe", nc.tensor), ("act", nc.scalar), ("dve", nc.vector),
                        ("pool", nc.gpsimd), ("sp", nc.sync)):
            sem = es.enter_context(nc.semaphore("sem_" + name))
            self.eng[name] = dict(h=h, sem=sem, cnt=0, waited={}, pend=False)
        self.slots = {}
        for q in ("sp", "pool"):
            sl = []
            for i in range(n_slots):
                sem = es.enter_context(nc.semaphore(f"dq_{q}{i}"))
                sl.append([sem, 0])
            self.slots[q] = dict(sl=sl, nxt=0)
        self.n_ins = 0

    def buf(self, name=""):
        return Buf(name)

    def bufs(self, n, name=""):
        return [Buf(f"{name}{i}") for i in range(n)]

    def _wait(self, eng, deps):
        E = self.eng[eng]
        best = {}
        for (sem, val, src) in deps:
            if src == eng and eng == "pe":
                continue
            k = id(sem)
            if E["waited"].get(k, 0) >= val:
                continue
            if k not in best or best[k][1] < val:
                best[k] = (sem, val)
        for k, (sem, val) in best.items():
            E["h"].wait_ge(sem, val)
            E["waited"][k] = val
            self.n_ins += 1

    @staticmethod
    def _deps(reads, writes):
        deps = []
        for b in reads:
            deps.extend(b.w.values())
        for b in writes:
            deps.extend(b.w.values())
            deps.extend(b.r.values())
        return deps

    @staticmethod
    def _commit(ev, reads, writes):
        k = id(ev[0])
        for b in writes:
            if k not in b.w or b.w[k][1] < ev[1]:
                b.w[k] = ev
            b.r = {}
        for b in reads:
            if k not in b.r or b.r[k][1] < ev[1]:
                b.r[k] = ev

    def op(self, eng, reads, writes, fn, inc=True):
        E = self.eng[eng]
        self._wait(eng, self._deps(reads, writes))
        ins = fn(E["h"])
        self.n_ins += 1
        if inc:
            E["cnt"] += 1
            ins.then_inc(E["sem"], 1)
            ev = (E["sem"], E["cnt"], eng)
            E["pend"] = False
        else:
            ev = (E["sem"], E["cnt"] + 1, eng)
            E["pend"] = True
        self._commit(ev, reads, writes)
        return ev

    def dma(self, q, out, in_, reads, writes):
        E = self.eng[q]
        SL = self.slots[q]
        slot = SL["sl"][SL["nxt"]]
        SL["nxt"] = (SL["nxt"] + 1) % len(SL["sl"])
        deps = self._deps(reads, writes)
        if slot[1] > 0:
            deps.append((slot[0], slot[1], "dma"))
        self._wait(q, deps)
        ins = E["h"].dma_start(out=out, in_=in_)
        self.n_ins += 1
        slot[1] += 16
        ins.then_inc(slot[0], 16)
        ev = (slot[0], slot[1], "dma")
        self._commit(ev, reads, writes)
        return ev

    def barrier(self):
        evs = []
        for n, E in self.eng.items():
            assert not E["pend"], n
            if E["cnt"] > 0:
                evs.append((E["sem"], E["cnt"], n))
        for q, SL in self.slots.items():
            for sem, val in SL["sl"]:
                if val > 0:
                    evs.append((sem, val, "dma"))
        for n in self.eng:
            self._wait(n, [e for e in evs if e[2] != n or n != "pe"])


def _rope_tables(n, head_dim, grid_w):
    rows = n // grid_w
    row = np.repeat(np.arange(rows, dtype=np.float32), grid_w)
    col = np.tile(np.arange(grid_w, dtype=np.float32), rows)
    n_freq = head_dim // 4
    inv = (np.float32(10000.0) ** (-np.arange(n_freq, dtype=np.float32) / np.float32(n_freq))).astype(np.float32)
    ang = np.concatenate([row[:, None] * inv, col[:, None] * inv], axis=-1).astype(np.float32)
    return np.cos(ang).astype(np.float32), np.sin(ang).astype(np.float32)


def make_consts(cfg):
    c = {}
    c["ident"] = np.eye(128, dtype=np.float32)
    c["ones"] = np.ones((128, 128), np.float32)
    pt = np.zeros((128, 128), np.float32)
    for i in range(64):
        pt[2 * i + 1, 2 * i] = -1.0
        pt[2 * i, 2 * i + 1] = 1.0
    c["protT"] = pt
    m = np.arange(128, dtype=np.float32)[:, None]
    n = np.arange(128, dtype=np.float32)[None, :]
    mk = np.zeros((128, 4, 128), np.float32)
    mk[:, 0] = np.maximum(n - m, 0)
    mk[:, 1] = (n >= m)
    mk[:, 2] = np.maximum(m - n, 0)
    mk[:, 3] = (m >= n)
    c["maskc"] = mk
    p = np.arange(128, dtype=np.float32)
    c["idxc"] = np.stack([p + 1, 128 - p, 127 - p, p, np.full(128, 128.0, np.float32)], 1).astype(np.float32)
    cr, sr = _rope_tables(cfg.SEQ, 256, cfg.GRID_W)
    f = np.arange(256)
    tab = np.stack([cr[:, f // 2].T, sr[:, f // 2].T], 0)
    c["ropeR"] = np.ascontiguousarray(tab.reshape(2, 2, 128, cfg.SEQ).transpose(0, 2, 1, 3))
    ca, sa = _rope_tables(cfg.SEQ, 128, cfg.GRID_W)
    f = np.arange(128)
    c["ropeA"] = np.ascontiguousarray(np.stack([ca[:, f // 2].T, sa[:, f // 2].T], 0))
    return c


def prep_weights(cfg, I):
    KC, FC, RH, AKV, D, FF = cfg.KC, cfg.FC, cfg.RH, cfg.AKV, cfg.D, cfg.FF
    f32 = lambda a: np.ascontiguousarray(np.asarray(a, dtype=np.float32))
    W = {}
    W["adaW"] = f32(np.asarray(I["ada_w"]).reshape(2, KC, 128, cfg.NB, 512).transpose(0, 3, 2, 1, 4))
    W["adaB"] = f32(np.asarray(I["ada_b"]).reshape(2, 6 * KC, 128).transpose(2, 0, 1))
    W["normW"] = f32(np.asarray(I["norm_w"]).reshape(2, 2, KC, 128).transpose(3, 0, 1, 2))
    W["fnW"] = f32(np.asarray(I["final_norm_w"]).reshape(KC, 128).T)
    rw = np.asarray(I["ret_w_in"])[0]
    q, k, v, g = rw[:, 0:D], rw[:, D:2 * D], rw[:, 2 * D:4 * D], rw[:, 4 * D:6 * D]
    blocks = []
    for h in range(RH):
        qk = np.concatenate([q[:, 256 * h:256 * h + 256], k[:, 256 * h:256 * h + 256]], 1)
        blocks.append(np.stack([qk, v[:, 512 * h:512 * h + 512], g[:, 512 * h:512 * h + 512]], 0))
    rb = np.stack(blocks, 0)
    W["retWin"] = f32(rb.reshape(RH, 3, KC, 128, 512).transpose(0, 1, 3, 2, 4))
    W["retWout"] = f32(np.asarray(I["ret_w_out"])[0].reshape(2 * KC, 128, KC, 128).transpose(2, 1, 0, 3))
    W["retLD"] = f32(np.asarray(I["ret_log_decay"])[0].reshape(1, 2 * RH))
    W["retGN"] = f32(np.asarray(I["ret_gn_w"])[0].reshape(1, 2 * D))
    aw = np.asarray(I["attn_w_in"])[0]
    W["attWq"] = f32(aw[:, 0:D].reshape(KC, 128, AKV, 512).transpose(2, 1, 0, 3))
    W["attWk"] = f32(aw[:, D:D + cfg.KVW].reshape(KC, 128, cfg.KVW).transpose(1, 0, 2))
    W["attWv"] = f32(aw[:, D + cfg.KVW:D + 2 * cfg.KVW].reshape(KC, 128, cfg.KVW).transpose(1, 0, 2))
    W["attWout"] = f32(np.asarray(I["attn_w_out"])[0].reshape(KC, 128, KC, 128).transpose(2, 1, 0, 3))
    W["attQN"] = f32(np.asarray(I["attn_q_norm"])[0].reshape(128, 1))
    W["attKN"] = f32(np.asarray(I["attn_k_norm"])[0].reshape(128, 1))
    W["ffnUp"] = f32(np.asarray(I["ffn_w_up"]).reshape(2, KC, 128, 2, FC, 128).transpose(0, 4, 2, 1, 3, 5)
                     .reshape(2, FC, 128, KC, 256))
    W["convW"] = f32(np.asarray(I["ffn_conv_w"]).reshape(2, 3, 2, FC, 128).transpose(4, 0, 3, 2, 1))
    W["convB"] = f32(np.asarray(I["ffn_conv_b"]).reshape(2, 2, FC, 128).transpose(3, 0, 2, 1))
    W["ffnDown"] = f32(np.asarray(I["ffn_w_down"]).reshape(2, FC, 128, KC, 128).transpose(0, 3, 2, 1, 4))
    return W


def prep_core(cfg, I, b):
    KC, SEQ, CTX, D = cfg.KC, cfg.SEQ, cfg.CTX, cfg.D
    xT = np.zeros((D, cfg.TOT), np.float32)
    xT[:, 1:SEQ + 1] = np.asarray(I["x"])[b].T
    xT[:, SEQ + 3:SEQ + 3 + CTX] = np.asarray(I["ctx"])[b].T
    m = {}
    m["xT"] = np.ascontiguousarray(xT.reshape(KC, 128, cfg.TOT).transpose(1, 0, 2))
    cT = np.stack([np.asarray(I["c"])[b], np.asarray(I["c_ctx"])], -1).astype(np.float32)
    m["cT"] = np.ascontiguousarray(cT.reshape(KC, 128, 2).transpose(1, 0, 2))
    return m


def prep_half(cfg, consts, r):
    m = {}
    sel = np.zeros((128, 4), np.float32)
    sel[:, 0] = 1 - r
    sel[:, 1] = r
    sel[:, 2] = r
    sel[:, 3] = 1 - r
    m["rsel"] = sel
    t = np.clip(r * cfg.HALF - 1 + np.arange(cfg.NQ), 0, cfg.SEQ - 1)
    m["ropeAw"] = np.ascontiguousarray(consts["ropeA"][:, :, t])
    return m


INPUT_SHAPES = lambda cfg: dict(
    xT=[128, cfg.KC, cfg.TOT], cT=[128, cfg.KC, 2],
    adaW=[2, cfg.NB, 128, cfg.KC, 512], adaB=[128, 2, 6 * cfg.KC], normW=[128, 2, 2, cfg.KC], fnW=[128, cfg.KC],
    retWin=[cfg.RH, 3, 128, cfg.KC, 512], retWout=[cfg.KC, 128, 2 * cfg.KC, 128], retLD=[1, 2 * cfg.RH],
    retGN=[1, 2 * cfg.D],
    attWq=[cfg.AKV, 128, cfg.KC, 512], attWk=[128, cfg.KC, cfg.KVW], attWv=[128, cfg.KC, cfg.KVW],
    attWout=[cfg.KC, 128, cfg.KC, 128], attQN=[128, 1], attKN=[128, 1],
    ffnUp=[2, cfg.FC, 128, cfg.KC, 256], convW=[128, 2, cfg.FC, 2, 3], convB=[128, 2, cfg.FC, 2],
    ffnDown=[2, cfg.KC, 128, cfg.FC, 128],
    ident=[128, 128], ones=[128, 128], protT=[128, 128], maskc=[128, 4, 128], idxc=[128, 5],
    ropeR=[2, 128, 2, cfg.SEQ], ropeA=[2, 128, cfg.SEQ],
    ropeAw=[2, 128, cfg.NQ], rsel=[128, 4],
)


class Builder:
    def __init__(self, cfg, stop_after=None, debug_x1=False):
        self.cfg = cfg
        self.stop_after = stop_after
        self.debug_x1 = debug_x1
        self.nc = bass.Bass("TRN2", target_bir_lowering=False)
        nc = self.nc
        self.I = {n: nc.dram_tensor(n, s, F32, kind="ExternalInput").ap() for n, s in INPUT_SHAPES(cfg).items()}
        self.outT = nc.dram_tensor("outT", [128, cfg.KC, cfg.HALF], F32, kind="ExternalOutput").ap()
        self.xwT = nc.dram_tensor("xwT", [128, cfg.KC, cfg.NQ], F32, kind="Internal").ap()
        self.hxwT = nc.dram_tensor("hxwT", [128, cfg.KC, cfg.NQ], BF16, kind="Internal").ap()
        if debug_x1:
            self.dbgT = nc.dram_tensor("dbgT", [128, cfg.KC, cfg.TOT], F32, kind="ExternalOutput").ap()
        self.hxT = nc.dram_tensor("hxT", [128, cfg.KC, cfg.TOT], BF16, kind="Internal").ap()
        self.zT = nc.dram_tensor("zT", [128, 2 * cfg.KC, cfg.TOT], BF16, kind="Internal").ap()
        self.x1T = nc.dram_tensor("x1T", [128, cfg.KC, cfg.TOT], F32, kind="Internal").ap()
        self.WB = dict(
            retWout=nc.dram_tensor("retWoutB", [cfg.KC, 128, 2 * cfg.KC, 128], BF16, kind="Internal").ap(),
            attWout=nc.dram_tensor("attWoutB", [cfg.KC, 128, cfg.KC, 128], BF16, kind="Internal").ap(),
            ffnUp=nc.dram_tensor("ffnUpB", [2, cfg.FC, 128, cfg.KC, 256], BF16, kind="Internal").ap(),
            ffnDown=nc.dram_tensor("ffnDownB", [2, cfg.KC, 128, cfg.FC, 128], BF16, kind="Internal").ap(),
        )

    def sb(self, es, name, shape, dt):
        self._uid = getattr(self, "_uid", 0) + 1
        return es.enter_context(self.nc.sbuf_tensor(f"sb{self._uid}_{name}", shape, dt))

    def seq_col(self, s):
        c = self.cfg
        return c.SEQ + 3 + s if s < c.CTX else s - c.CTX + 1

    def make_conv_jobs(self):
        cfg, S = self.cfg, self.S
        self.conv_jobs = []
        self.conv_buf = {}

        def add(kind, idx):
            b = S.buf(f"cv_{kind}{idx}")
            self.conv_buf[(kind,) + idx] = b
            self.conv_jobs.append((self.WB[kind][idx], self.I[kind][idx], b))
        for f in range(cfg.KC):
            add("retWout", (f,))
        for j in range(cfg.FC):
            add("ffnUp", (0, j))
        for f in range(cfg.KC):
            add("ffnDown", (0, f))
        for f in range(cfg.KC):
            add("attWout", (f,))
        for j in range(cfg.FC):
            add("ffnUp", (1, j))
        for f in range(cfg.KC):
            add("ffnDown", (1, f))
        self.conv_total = len(self.conv_jobs)

    def emit_conv(self, n, gate=()):
        for _ in range(n):
            if not self.conv_jobs:
                return
            dst, src, b = self.conv_jobs.pop(0)
            self.S.dma("pool", dst, src, list(gate), [b])

    def build(self):
        cfg, nc = self.cfg, self.nc
        with ExitStack() as es:
            self.S = S = Sched(nc, es)
            self.make_conv_jobs()
            self.PS = [es.enter_context(nc.psum_tensor(f"ps{i}", [128, 512], F32)) for i in range(7)]
            self.PSB = [S.buf(f"ps{i}") for i in range(7)]
            self.PT = es.enter_context(nc.psum_tensor("pst", [128, 1024], BF16))
            self.PTB = [S.buf("pst0"), S.buf("pst1")]
            sb = lambda n, s, d: self.sb(es, n, s, d)
            self.ident = sb("ident", [128, 128], BF16)
            self.ones = sb("ones", [128, 128], BF16)
            self.protT = sb("protT", [128, 128], BF16)
            self.modT = sb("modT", [128, 2, 6 * cfg.KC, 2], F32)
            self.acoef = sb("acoef", [128, 2, 2, 2, cfg.KC], F32)
            self.normW = sb("normW", [128, 2, 2, cfg.KC], F32)
            self.fnW = sb("fnW", [128, cfg.KC], F32)
            self.convW = sb("convW", [128, 2, cfg.FC, 2, 3], F32)
            self.convB = sb("convB", [128, 2, cfg.FC, 2], F32)
            self.epsc = sb("epsc", [128, 1], F32)
            self.cact = sb("cact", [128, cfg.KC, 2], BF16)
            self.adab = sb("adab", [128, 2, 6 * cfg.KC], F32)
            self.B_cact, self.B_adab = S.buf("cact"), S.buf("adab")
            self.B_const = S.buf("const")
            self.B_mod = S.buf("mod")
            self.B_hxT, self.B_zT, self.B_x1T, self.B_out = S.buf("hxT"), S.buf("zT"), S.buf("x1T"), S.buf("out")
            self.B_xwT, self.B_hxwT = S.buf("xwT"), S.buf("hxwT")
            self.rsel = sb("rsel", [128, 4], F32)
            S.dma("sp", self.rsel[:], self.I["rsel"][:, :], [], [self.B_const])
            Bc = self.B_const
            S.dma("pool", self.ident[:], self.I["ident"][:, :], [], [Bc])
            S.dma("pool", self.ones[:], self.I["ones"][:, :], [], [Bc])
            S.dma("pool", self.protT[:], self.I["protT"][:, :], [], [Bc])
            S.dma("sp", self.normW[:], self.I["normW"][:, :, :, :], [], [Bc])
            S.dma("sp", self.fnW[:], self.I["fnW"][:, :], [], [Bc])
            S.dma("sp", self.convW[:], self.I["convW"][:, :, :, :, :], [], [Bc])
            S.dma("sp", self.convB[:], self.I["convB"][:, :, :, :], [], [Bc])
            S.op("dve", [], [Bc], lambda e: e.memset(self.epsc[:], EPS))

            np_ = getattr(self, "nphase", 99)
            self.phase_mods()
            S.barrier()
            if np_ >= 2:
                self.phase_norm1(0, self.I["xT"], None)
                S.barrier()
            if np_ >= 3:
                self.phase_retention()
                self.finish_mods(1)
                S.barrier()
            if np_ >= 4:
                self.phase_tokens(0)
                S.barrier()
            if self.stop_after != 0:
                self.phase_norm1(1, self.x1T, self.B_x1T)
                S.barrier()
                import os
                self.phase_window()
                S.barrier()
                if os.environ.get("DBG_STOPW") != "1":
                    self.phase_attention()
                    S.barrier()
                    if os.environ.get("DBG_STOPW") != "2":
                        self.phase_tokens(1)
                        S.barrier()
            S._wait("sp", list(self.B_out.w.values()))
        return nc

    def phase_mods(self):
        cfg, nc, S = self.cfg, self.nc, self.S
        KC = cfg.KC
        with ExitStack() as es:
            sb = lambda n, s, d: self.sb(es, n, s, d)
            c32 = sb("c32", [128, KC, 2], F32)
            wsl = [sb(f"adaw{i}", [128, KC, 512], BF16) for i in range(2)]
            Bw = S.bufs(2, "adaw")
            Bc32 = S.buf()
            cact, adab, Bcact, Bab = self.cact, self.adab, self.B_cact, self.B_adab
            S.dma("sp", c32[:], self.I["cT"][:, :, :], [], [Bc32])
            S.dma("sp", adab[:], self.I["adaB"][:, :, :], [], [Bab])
            S.op("act", [Bc32], [Bcact], lambda e: e.activation(out=cact[:], in_=c32[:], func=AF.Silu))
            l = 0
            for blk in range(cfg.NB):
                w, bw = wsl[blk % 2], Bw[blk % 2]
                S.dma("pool", w[:], self.I["adaW"][l, blk], [], [bw])
                for oc in range(4):
                    self.mod_block(l, blk * 4 + oc, w[:, :, oc * 128:(oc + 1) * 128], bw, (blk * 4 + oc) % 2)
            self.finish_mods(0)

    def mod_block(self, l, j, w128, bw, pi):
        S, KC = self.S, self.cfg.KC
        ps, pb = self.PS[pi], self.PSB[pi]
        for k in range(KC):
            S.op("pe", [bw, self.B_cact], [pb],
                 lambda e: e.matmul(ps[:, 0:2], w128[:, k, :], self.cact[:, k, :], start=(k == 0), stop=(k == KC - 1)), inc=(k == KC - 1))
        S.op("dve", [pb, self.B_adab], [self.B_mod],
             lambda e: e.tensor_scalar(out=self.modT[:, l, j, :], in0=ps[:, 0:2], scalar1=self.adab[:, l, j:j + 1], scalar2=None, op0=ALU.add))

    def finish_mods(self, l):
        S, KC = self.S, self.cfg.KC
        for sub in range(2):
            for s in range(2):
                sc = self.modT[:, l, (3 * sub + 1) * KC:(3 * sub + 2) * KC, s]
                S.op("dve", [self.B_mod, self.B_const], [self.B_mod],
                     lambda e: e.scalar_tensor_tensor(out=self.acoef[:, l, sub, s, :], in0=sc, scalar=1.0,
                                                      in1=self.normW[:, l, sub, :], op0=ALU.add, op1=ALU.mult))

    def coef(self, l, sub, s, k):
        KC = self.cfg.KC
        A = self.acoef[:, l, sub, s, k:k + 1]
        Bsh = self.modT[:, l, 3 * sub * KC + k, s:s + 1]
        G = self.modT[:, l, (3 * sub + 2) * KC + k, s:s + 1]
        return A, Bsh, G

    def norm_mod(self, T, xt, bx, n, out, bout, l, sub, s, col0=0):
        cfg, S = self.cfg, self.S
        KC = cfg.KC
        import os
        stg = int(os.environ.get("DBG_STAGE", "9"))
        pss, bss = self.PS[6], self.PSB[6]
        if stg < 2: return
        for k in range(KC):
            sq, bsq = T["sq"][k % len(T["sq"])], T["Bsq"][k % len(T["sq"])]
            S.op("act", [bx], [bsq], lambda e: e.activation(out=sq[:, :n], in_=xt[:, k, col0:col0 + n], func=AF.Square))
            S.op("pe", [bsq, self.B_const], [bss],
                 lambda e: e.matmul(pss[:, :n], self.ones[:], sq[:, :n], start=(k == 0), stop=(k == KC - 1)))
        rstd, br = T["rstd"], T["Brstd"]
        if stg < 3: return
        self.rstd_from(pss, bss, n, rstd, br, 1.0 / cfg.D)
        if stg < 4: return
        for k in range(KC):
            A, Bsh, _ = self.coef(l, sub, s, k)
            tmp, bt = T["tmp"][k % len(T["tmp"])], T["Btmp"][k % len(T["tmp"])]
            S.op("dve", [bx, br, self.B_mod], [bt],
                 lambda e: e.scalar_tensor_tensor(out=tmp[:, :n], in0=xt[:, k, col0:col0 + n], scalar=A,
                                                  in1=rstd[:, :n], op0=ALU.mult, op1=ALU.mult))
            if stg < 5: continue
            S.op("act", [bt, self.B_mod], [bout],
                 lambda e: e.activation(out=out[:, k, :n], in_=tmp[:, :n], func=AF.Identity, bias=Bsh, scale=1.0))

    def rstd_from(self, ps, bps, n, rstd, br, inv_n):
        S = self.S
        S.op("act", [bps, self.B_const], [br],
             lambda e: e.activation(out=rstd[:, :n], in_=ps[:, :n], func=AF.Sqrt, bias=self.epsc[:, 0:1], scale=inv_n))
        S.op("dve", [br], [br], lambda e: e.reciprocal(out=rstd[:, :n], in_=rstd[:, :n]))

    def norm_tmps(self, es, tag, ntmp=2, nsq=2):
        S = self.S
        sb = lambda n, s, d: self.sb(es, n, s, d)
        return dict(sq=[sb(f"{tag}sq{i}", [128, 512], BF16) for i in range(nsq)], Bsq=S.bufs(nsq),
                    tmp=[sb(f"{tag}tmp{i}", [128, 512], F32) for i in range(ntmp)], Btmp=S.bufs(ntmp),
                    rstd=sb(f"{tag}rstd", [128, 512], F32), Brstd=S.buf())

    def regions(self):
        cfg = self.cfg
        return [(0, cfg.SEQ, 0), (cfg.SEQ + 2, cfg.CTX, 1)]

    def phase_norm1(self, l, src, bsrc):
        cfg, S = self.cfg, self.S
        KC = cfg.KC
        with ExitStack() as es:
            sb = lambda n, s, d: self.sb(es, n, s, d)
            xts = [sb(f"n1x{i}", [128, KC, 512], F32) for i in range(2)]
            Bx = S.bufs(2)
            hts = [sb(f"n1h{i}", [128, KC, 512], BF16) for i in range(2)]
            Bh = S.bufs(2)
            T = self.norm_tmps(es, "n1", ntmp=8, nsq=4)
            tl = []
            for (base, L, s) in self.regions():
                c0 = base
                while c0 < base + L + 2:
                    n = min(512, base + L + 2 - c0)
                    tl.append((c0, n, s))
                    c0 += n

            def load(it):
                c0, n, s = tl[it]
                S.dma("sp", xts[it % 2][:, :, :n], src[:, :, c0:c0 + n], [bsrc] if bsrc else [], [Bx[it % 2]])

            load(0)
            for it, (c0, n, s) in enumerate(tl):
                if it + 1 < len(tl):
                    load(it + 1)
                xt, bx, ht, bh = xts[it % 2], Bx[it % 2], hts[it % 2], Bh[it % 2]
                self.norm_mod(T, xt, bx, n, ht, bh, l, 0, s)
                S.dma("sp", self.hxT[:, :, c0:c0 + n], ht[:, :, :n], [bh], [self.B_hxT])

    def phase_tokens(self, l):
        cfg, S = self.cfg, self.S
        KC, FC = cfg.KC, cfg.FC
        ZC = 2 * KC if l == 0 else KC
        wkind = "retWout" if l == 0 else "attWout"
        wout = self.WB[wkind]
        last = (l == 1)
        regions = self.regions() if l == 0 else [(0, cfg.NQ - 2, 0)]
        with ExitStack() as es:
            sb = lambda n, s, d: self.sb(es, n, s, d)
            UW = max(ZC, FC) * 512
            U = sb("tkU", [128, UW], BF16)
            zt = U[:, 0:ZC * 512].rearrange("p (c n) -> p c n", n=512)
            act = U[:, 0:FC * 512].rearrange("p (c n) -> p c n", n=512)
            BU = S.buf("U")
            xt = sb("tkx", [128, KC, 512], F32)
            Bx = S.buf("xt")
            h2 = sb("tkh2", [128, KC, 512], BF16)
            Bh2 = S.buf("h2")
            wo = [sb(f"tkwo{i}", [128, ZC, 128], BF16) for i in range(3)]
            Bwo = S.bufs(3)
            wu = [sb(f"tkwu{i}", [128, KC, 256], BF16) for i in range(3)]
            Bwu = S.bufs(3)
            wd = [sb(f"tkwd{i}", [128, FC, 128], BF16) for i in range(2)]
            Bwd = S.bufs(2)
            T = self.norm_tmps(es, "tk")
            ca = [sb(f"tkca{i}", [128, 512], F32) for i in range(2)]
            cb = [sb(f"tkcb{i}", [128, 512], F32) for i in range(2)]
            sa = [sb("tksa0", [128, 512], F32)] * 2
            Bca, Bcb, Bsa = S.bufs(2), S.bufs(2), [S.buf()] * 2
            if last:
                ot = sb("tkout", [128, KC, 512], F32)
                Bot = S.buf("ot")
            iwo = iwu = iwd = 0
            for (base, L, s) in regions:
                w0 = 0
                nwin = -(-L // 510)
                widths = [L // nwin + (1 if i < L % nwin else 0) for i in range(nwin)]
                for NV in widths:
                    N = NV + 2
                    c0 = base + w0
                    first_win = (base == 0 and w0 == 0)
                    S.dma("sp", zt[:, :, :N], self.zT[:, 0:ZC, c0:c0 + N], [self.B_zT], [BU])
                    xsrc, bxs = (self.I["xT"], []) if l == 0 else (self.xwT, [self.B_xwT])
                    S.dma("sp", xt[:, :, :N], xsrc[:, :, c0:c0 + N], bxs, [Bx])
                    padcols = ([0] if w0 == 0 else []) + ([N - 1] if w0 + N == L + 2 else [])
                    edge = [(pc, 2 if pc == 0 else 3) for pc in padcols]
                    for pc in (padcols if l == 0 else []):
                        S.op("dve", [], [BU], lambda e: e.memset(zt[:, :, pc:pc + 1], 0.0))
                        S.op("dve", [], [Bx], lambda e: e.memset(xt[:, :, pc:pc + 1], 0.0))
                    for f in range(KC):
                        w, bw = wo[iwo % 3], Bwo[iwo % 3]
                        iwo += 1
                        cvb = self.conv_buf[(wkind, f)]
                        if first_win:
                            S.dma("pool", w[:], self.I[wkind][f], [], [bw])
                            S.dma("sp", wout[f], w[:], [bw], [cvb])
                        else:
                            S.dma("pool", w[:], wout[f], [cvb], [bw])
                        ps, pb = self.PS[f % 2], self.PSB[f % 2]
                        for k in range(ZC):
                            S.op("pe", [bw, BU], [pb],
                                 lambda e: e.matmul(ps[:, :N], w[:, k, :], zt[:, k, :N], start=(k == 0), stop=(k == ZC - 1)),
                                 inc=(k == ZC - 1))
                        _, _, G = self.coef(l, 0, s, f)
                        S.op("dve", [pb, Bx, self.B_mod], [Bx],
                             lambda e: e.scalar_tensor_tensor(out=xt[:, f, :N], in0=ps[:, :N], scalar=G,
                                                              in1=xt[:, f, :N], op0=ALU.mult, op1=ALU.add))
                    self.norm_mod(T, xt, Bx, N, h2, Bh2, l, 1, s)
                    for pc, mi in edge:
                        if l == 0:
                            S.op("dve", [], [Bh2], lambda e: e.memset(h2[:, :, pc:pc + 1], 0.0))
                        else:
                            for k in range(KC):
                                S.op("act", [Bh2, self.B_const], [Bh2],
                                     lambda e: e.activation(out=h2[:, k, pc:pc + 1], in_=h2[:, k, pc:pc + 1], func=AF.Copy,
                                                            scale=self.rsel[:, mi:mi + 1]))
                    for j in range(FC):
                        w, bw = wu[iwu % 3], Bwu[iwu % 3]
                        iwu += 1
                        cvb = self.conv_buf[("ffnUp", l, j)]
                        if first_win:
                            S.dma("pool", w[:], self.I["ffnUp"][l, j], [], [bw])
                            S.dma("sp", self.WB["ffnUp"][l, j], w[:], [bw], [cvb])
                        else:
                            S.dma("pool", w[:], self.WB["ffnUp"][l, j], [cvb], [bw])
                        pa, pab = self.PS[2 + (j % 2) * 2], self.PSB[2 + (j % 2) * 2]
                        pbb, pbbb = self.PS[3 + (j % 2) * 2], self.PSB[3 + (j % 2) * 2]
                        for k in range(KC):
                            S.op("pe", [bw, Bh2], [pab],
                                 lambda e: e.matmul(pa[:, :N], w[:, k, 0:128], h2[:, k, :N], start=(k == 0), stop=(k == KC - 1)),
                                 inc=(k == KC - 1))
                        for k in range(KC):
                            S.op("pe", [bw, Bh2], [pbbb],
                                 lambda e: e.matmul(pbb[:, :N], w[:, k, 128:256], h2[:, k, :N], start=(k == 0), stop=(k == KC - 1)),
                                 inc=(k == KC - 1))
                        for (pp, ppb, cc, ccb, ab) in ((pa, pab, ca[j % 2], Bca[j % 2], 0), (pbb, pbbb, cb[j % 2], Bcb[j % 2], 1)):
                            cw = lambda t: self.convW[:, l, j, ab, t:t + 1]
                            S.op("act", [ppb, self.B_const], [ccb],
                                 lambda e: e.activation(out=cc[:, :NV], in_=pp[:, 1:N - 1], func=AF.Identity,
                                                        bias=self.convB[:, l, j, ab:ab + 1], scale=cw(1)))
                            S.op("dve", [ppb, ccb, self.B_const], [ccb],
                                 lambda e: e.scalar_tensor_tensor(out=cc[:, :NV], in0=pp[:, 0:N - 2], scalar=cw(0),
                                                                  in1=cc[:, :NV], op0=ALU.mult, op1=ALU.add))
                            S.op("dve", [ppb, ccb, self.B_const], [ccb],
                                 lambda e: e.scalar_tensor_tensor(out=cc[:, :NV], in0=pp[:, 2:N], scalar=cw(2),
                                                                  in1=cc[:, :NV], op0=ALU.mult, op1=ALU.add))
                        S.op("act", [Bca[j % 2]], [Bsa[j % 2]],
                             lambda e: e.activation(out=sa[j % 2][:, :NV], in_=ca[j % 2][:, :NV], func=AF.Silu))
                        S.op("dve", [Bsa[j % 2], Bcb[j % 2]], [BU],
                             lambda e: e.tensor_tensor(out=act[:, j, :NV], in0=sa[j % 2][:, :NV], in1=cb[j % 2][:, :NV], op=ALU.mult))
                    for f in range(KC):
                        w, bw = wd[iwd % 2], Bwd[iwd % 2]
                        iwd += 1
                        cvb = self.conv_buf[("ffnDown", l, f)]
                        if first_win:
                            S.dma("pool", w[:], self.I["ffnDown"][l, f], [], [bw])
                            S.dma("sp", self.WB["ffnDown"][l, f], w[:], [bw], [cvb])
                        else:
                            S.dma("pool", w[:], self.WB["ffnDown"][l, f], [cvb], [bw])
                        ps, pb = self.PS[f % 2], self.PSB[f % 2]
                        for j in range(FC):
                            S.op("pe", [bw, BU], [pb],
                                 lambda e: e.matmul(ps[:, :NV], w[:, j, :], act[:, j, :NV], start=(j == 0), stop=(j == FC - 1)),
                                 inc=(j == FC - 1))
                        _, _, G = self.coef(l, 1, s, f)
                        S.op("dve", [pb, Bx, self.B_mod], [Bx],
                             lambda e: e.scalar_tensor_tensor(out=xt[:, f, 1:N - 1], in0=ps[:, :NV], scalar=G,
                                                              in1=xt[:, f, 1:N - 1], op0=ALU.mult, op1=ALU.add))
                    if not last:
                        lo = 0 if w0 == 0 else 1
                        hi = N if w0 + N == L + 2 else N - 1
                        S.dma("sp", self.x1T[:, :, c0 + lo:c0 + hi], xt[:, :, lo:hi], [Bx], [self.B_x1T])
                        if self.debug_x1:
                            S.dma("sp", self.dbgT[:, :, c0 + 1:c0 + 1 + NV], xt[:, :, 1:N - 1], [Bx], [self.B_out])
                    else:
                        pss, bss = self.PS[6], self.PSB[6]
                        for k in range(KC):
                            sq, bsq = T["sq"][k % 2], T["Bsq"][k % 2]
                            S.op("act", [Bx], [bsq], lambda e: e.activation(out=sq[:, :NV], in_=xt[:, k, 1:N - 1], func=AF.Square))
                            S.op("pe", [bsq, self.B_const], [bss],
                                 lambda e: e.matmul(pss[:, :NV], self.ones[:], sq[:, :NV], start=(k == 0), stop=(k == KC - 1)))
                        rstd, br = T["rstd"], T["Brstd"]
                        self.rstd_from(pss, bss, NV, rstd, br, 1.0 / cfg.D)
                        for k in range(KC):
                            S.op("dve", [Bx, br, self.B_const], [Bot],
                                 lambda e: e.scalar_tensor_tensor(out=ot[:, k, :NV], in0=xt[:, k, 1:N - 1], scalar=self.fnW[:, k:k + 1],
                                                                  in1=rstd[:, :NV], op0=ALU.mult, op1=ALU.mult))
                        S.dma("sp", self.outT[:, :, w0:w0 + NV], ot[:, :, :NV], [Bot], [self.B_out])
                    w0 += NV

    def phase_retention(self):
        cfg, S = self.cfg, self.S
        KC, RH, NCH, NT, CTX, SEQ = cfg.KC, cfg.RH, cfg.NCH, cfg.NT, cfg.CTX, cfg.SEQ
        CC = CTX // 128
        with ExitStack() as es:
            sb = lambda n, s, d: self.sb(es, n, s, d)
            ld = sb("r_ld", [128, 2 * RH], F32)
            lg = sb("r_lg", [128, 2 * RH], F32)
            dtab = sb("r_dtab", [128, 5, 2 * RH], F32)
            idxc = sb("r_idx", [128, 5], F32)
            maskc = sb("r_maskc", [128, 4, 128], F32)
            Bk = S.buf("rconst")
            S.dma("sp", ld[:], self.I["retLD"][0:1, :].partition_broadcast(128), [], [Bk])
            S.dma("sp", idxc[:], self.I["idxc"][:, :], [], [Bk])
            S.dma("sp", maskc[:], self.I["maskc"][:, :, :], [], [Bk])
            S.op("act", [Bk], [Bk], lambda e: e.activation(out=lg[:], in_=ld[:], func=AF.Exp))
            S.op("dve", [Bk], [Bk], lambda e: e.tensor_scalar(out=lg[:], in0=lg[:], scalar1=-1.0, scalar2=None, op0=ALU.mult))
            for t in range(5):
                S.op("dve", [Bk], [Bk], lambda e: e.tensor_scalar(out=dtab[:, t, :], in0=lg[:], scalar1=idxc[:, t:t + 1],
                                                                  scalar2=None, op0=ALU.mult))
            S.op("act", [Bk], [Bk], lambda e: e.activation(out=dtab[:], in_=dtab[:], func=AF.Exp))

            HXW = 256
            hxs = [sb(f"r_hx{i}", [128, KC, HXW], BF16) for i in range(2)]
            Bhx = S.bufs(2)
            ws = [sb(f"r_w{i}", [128, KC, 512], BF16) for i in range(3)]
            Bw = S.bufs(3)
            ropes = [sb("r_rope0", [128, 2, 256], F32)]
            Brope = S.bufs(1)
            qT = sb("r_qT", [128, 2, NT], BF16)
            kT = sb("r_kT", [128, 2, NT], BF16)
            v = sb("r_v", [128, NCH, 512], BF16)
            sgw = sb("r_sgw", [128, NCH, 512], BF16)
            yb = sb("r_yb", [128, NCH, 512], BF16)
            BqT, BkT = S.bufs(NCH, "qT"), S.bufs(NCH, "kT")
            Bv, Bsgw, Byb = S.bufs(NCH, "v"), S.bufs(NCH, "sgw"), S.bufs(NCH, "yb")
            gnw = sb("r_gnw", [128, 512], F32)
            Bgnw = S.buf()
            mask = sb("r_mask", [128, 128], F32)
            mtmp = sb("r_mtmp", [128, 128], F32)
            Bmask = S.buf()
            raw = [sb("r_raw0", [128, 256], BF16)] * 2
            Braw = [S.buf()] * 2
            t1 = [sb("r_t10", [128, 512], F32)] * 2
            Bt1 = [S.buf()] * 2
            t2 = [sb("r_t20", [128, 256], F32)] * 2
            Bt2 = [S.buf()] * 2
            kd = {d: [sb(f"r_kd{d}{i}", [128, 256], BF16) for i in range(2)] for d in (0, 1)}
            Bkd = {d: S.bufs(2) for d in (0, 1)}
            Bptk = {0: S.buf("ptk0"), 1: S.buf("ptk1")}
            sT = [sb(f"r_sT{i}", [128, 128], BF16) for i in range(2)]
            BsT = S.bufs(2)
            st32 = {d: [sb(f"r_st32{d}{i}", [128, 2, 512], F32) for i in range(2)] for d in (0, 1)}
            st16 = {d: [sb(f"r_st16{d}{i}", [128, 2, 512], BF16) for i in range(3)] for d in (0, 1)}
            Bst32 = {d: S.bufs(2) for d in (0, 1)}
            Bst16 = {d: S.bufs(3) for d in (0, 1)}
            yy = [sb(f"r_y{i}", [128, 512], F32) for i in range(4)]
            By = S.bufs(4)
            stat = [sb(f"r_stat{i}", [128, 8], F32) for i in range(4)]
            Bstat = S.bufs(4)
            zz = [sb(f"r_z{i}", [128, 512], BF16) for i in range(2)]
            Bz = S.bufs(2)
            NZ = 4
            adw = [sb(f"r_adw{i}", [128, KC, 128], BF16) for i in range(2)]
            Badw = S.bufs(2)
            nmod = 6 * KC
            per_head = -(-nmod // RH)
            modq = {"issued": 0, "done": 0}

            def mod_issue():
                jn = modq["issued"]
                if jn < nmod:
                    blk, oc = jn // 4, jn % 4
                    S.dma("pool", adw[jn % 2][:], self.I["adaW"][1, blk][:, :, oc * 128:(oc + 1) * 128], [], [Badw[jn % 2]])
                    modq["issued"] += 1

            def mod_compute():
                jn = modq["done"]
                if jn < modq["issued"]:
                    self.mod_block(1, jn, adw[jn % 2], Badw[jn % 2], 6)
                    modq["done"] += 1
            zTt = [sb(f"r_zT{i}", [128, 4, 128], BF16) for i in range(NZ)]
            BzT = S.bufs(NZ)
            Bgate = S.buf("gate")

            tiles = []
            s0 = 0
            while s0 < NT:
                lim = CTX if s0 < CTX else NT
                n = min(HXW, lim - s0)
                tiles.append((s0, n))
                s0 += n
            ihx = iw = irope = iraw = 0
            for h in range(RH):
                dcol = lambda d: d * RH + h
                S.dma("sp", gnw[:], self.I["retGN"][0:1, 512 * h:512 * h + 512].partition_broadcast(128), [], [Bgnw])
                S.op("act", [Bk], [Bmask], lambda e: e.activation(out=mask[:], in_=maskc[:, 0, :], func=AF.Exp,
                                                                  scale=lg[:, dcol(0):dcol(0) + 1]))
                S.op("dve", [Bmask, Bk], [Bmask], lambda e: e.tensor_tensor(out=mask[:], in0=mask[:], in1=maskc[:, 1, :], op=ALU.mult))
                S.op("act", [Bk, Bmask], [Bmask], lambda e: e.activation(out=mtmp[:], in_=maskc[:, 2, :], func=AF.Exp,
                                                                         scale=lg[:, dcol(1):dcol(1) + 1]))
                S.op("dve", [Bmask, Bk], [Bmask], lambda e: e.tensor_tensor(out=mtmp[:], in0=mtmp[:], in1=maskc[:, 3, :], op=ALU.mult))
                S.op("dve", [Bmask], [Bmask], lambda e: e.tensor_tensor(out=mask[:], in0=mask[:], in1=mtmp[:], op=ALU.add))
                wb = []
                for j in range(3):
                    w, bw = ws[iw % 3], Bw[iw % 3]
                    iw += 1
                    S.dma("pool", w[:], self.I["retWin"][h, j], [], [bw])
                    wb.append((w, bw))
                (wqk, bwqk), (wv, bwv), (wg, bwg) = wb
                for (s0, n) in tiles:
                    hx, bhx = hxs[ihx % 2], Bhx[ihx % 2]
                    ihx += 1
                    col = self.seq_col(s0)
                    S.dma("sp", hx[:, :, :n], self.hxT[:, :, col:col + n], [self.B_hxT], [bhx])
                    lat = s0 >= CTX
                    chs = list(range(s0 // 128, (s0 + n) // 128))
                    rp, brp = ropes[0], Brope[0]
                    t0 = s0 - CTX
                    for oc in (0, 2, 1, 3):
                        dst, bdst = (qT, BqT) if oc < 2 else (kT, BkT)
                        j = oc % 2
                        if lat and oc < 2:
                            for a_ in range(2):
                                S.dma("sp", rp[:, a_, :n], self.I["ropeR"][a_, :, j, t0:t0 + n], [], [brp])
                        scale = 1.0 if oc < 2 else 1.0 / 16.0
                        ps, pb = self.PS[oc % 2], self.PSB[oc % 2]
                        for k in range(KC):
                            S.op("pe", [bwqk, bhx], [pb],
                                 lambda e: e.matmul(ps[:, :n], wqk[:, k, oc * 128:(oc + 1) * 128], hx[:, k, :n],
                                                    start=(k == 0), stop=(k == KC - 1)), inc=(k == KC - 1))
                        wr = [bdst[c] for c in chs]
                        if not lat:
                            S.op("act", [pb], wr, lambda e: e.activation(out=dst[:, j, s0:s0 + n], in_=ps[:, :n], func=AF.Copy, scale=scale))
                        else:
                            rw_, brw = raw[iraw % 2], Braw[iraw % 2]
                            a1, ba1 = t1[iraw % 2], Bt1[iraw % 2]
                            a2, ba2 = t2[iraw % 2], Bt2[iraw % 2]
                            iraw += 1
                            S.op("act", [pb], [brw], lambda e: e.activation(out=rw_[:, :n], in_=ps[:, :n], func=AF.Copy, scale=scale))
                            pr, pbr = self.PS[2], self.PSB[2]
                            S.op("pe", [brw, self.B_const], [pbr], lambda e: e.matmul(pr[:, :n], self.protT[:], rw_[:, :n], start=True, stop=True))
                            S.op("dve", [brw, brp], [ba1], lambda e: e.tensor_tensor(out=a1[:, :n], in0=rw_[:, :n], in1=rp[:, 0, :n], op=ALU.mult))
                            S.op("dve", [pbr, brp], [ba2], lambda e: e.tensor_tensor(out=a2[:, :n], in0=pr[:, :n], in1=rp[:, 1, :n], op=ALU.mult))
                            S.op("dve", [ba1, ba2], wr, lambda e: e.tensor_tensor(out=dst[:, j, s0:s0 + n], in0=a1[:, :n], in1=a2[:, :n], op=ALU.add))
                    for c in chs:
                        o = c * 128 - s0
                        ps, pb = self.PS[3], self.PSB[3]
                        for k in range(KC):
                            S.op("pe", [bwv, bhx], [pb], lambda e: e.matmul(ps[:, :], hx[:, k, o:o + 128], wv[:, k, :],
                                                                            start=(k == 0), stop=(k == KC - 1)), inc=(k == KC - 1))
                        S.op("act", [pb], [Bv[c]], lambda e: e.activation(out=v[:, c, :], in_=ps[:, :], func=AF.Copy))
                        ps, pb = self.PS[4], self.PSB[4]
                        for k in range(KC):
                            S.op("pe", [bwg, bhx], [pb], lambda e: e.matmul(ps[:, :], hx[:, k, o:o + 128], wg[:, k, :],
                                                                            start=(k == 0), stop=(k == KC - 1)), inc=(k == KC - 1))
                        a1, ba1 = t1[c % 2], Bt1[c % 2]
                        S.op("act", [pb], [ba1], lambda e: e.activation(out=a1[:, :], in_=ps[:, :], func=AF.Silu))
                        S.op("dve", [ba1, Bgnw], [Bsgw[c]], lambda e: e.tensor_tensor(out=sgw[:, c, :], in0=a1[:, :], in1=gnw[:, :], op=ALU.mult))

                fwd = list(range(NCH))
                bwd = list(range(CC - 1, -1, -1)) + list(range(NCH - 1, CC - 1, -1))
                orders = {1: bwd, 0: fwd}
                pos = {d: {c: i for i, c in enumerate(orders[d])} for d in (0, 1)}
                for d in (0, 1):
                    S.op("dve", [], [Bst32[d][1]], lambda e: e.memset(st32[d][1][:], 0.0))
                    S.op("dve", [], [Bst16[d][0]], lambda e: e.memset(st16[d][0][:], 0.0))
                qd = {d: dtab[:, 0 if d == 0 else 1, dcol(d):dcol(d) + 1] for d in (0, 1)}
                kdc = {d: dtab[:, 2 if d == 0 else 3, dcol(d):dcol(d) + 1] for d in (0, 1)}
                cdec = {d: dtab[:, 4, dcol(d):dcol(d) + 1] for d in (0, 1)}
                head_target = min(nmod, (h + 1) * per_head)
                mod_issue()
                for i in range(NCH + 1):
                    if modq["done"] < head_target:
                        if modq["issued"] < head_target:
                            mod_issue()
                        mod_compute()
                    if i < NCH:
                        for d in (1, 0):
                            c = orders[d][i]
                            cs = slice(c * 128, (c + 1) * 128)
                            for j in range(2):
                                S.op("pe", [BkT[c], self.B_const], [Bptk[d]],
                                     lambda e: e.transpose(self.PT[:, d * 256 + j * 128:d * 256 + (j + 1) * 128], kT[:, j, cs], self.ident[:]),
                                     inc=(j == 1))
                        for d in (1, 0):
                            kd_, bkd = kd[d][i % 2], Bkd[d][i % 2]
                            S.op("act", [Bptk[d], Bk], [bkd], lambda e: e.activation(out=kd_[:], in_=self.PT[:, d * 256:(d + 1) * 256],
                                                                                     func=AF.Copy, scale=kdc[d]))
                        for d in (1, 0):
                            c = orders[d][i]
                            kd_, bkd = kd[d][i % 2], Bkd[d][i % 2]
                            for j in range(2):
                                pU, pUb = self.PS[(1 - d) * 2 + j], self.PSB[(1 - d) * 2 + j]
                                S.op("pe", [bkd, Bv[c]], [pUb], lambda e: e.matmul(pU[:, :], kd_[:, j * 128:(j + 1) * 128], v[:, c, :], start=True, stop=True))
                        for d in (1, 0):
                            for j in range(2):
                                pU, pUb = self.PS[(1 - d) * 2 + j], self.PSB[(1 - d) * 2 + j]
                                S.op("dve", [pUb, Bst32[d][(i + 1) % 2], Bk], [Bst32[d][i % 2]],
                                     lambda e: e.scalar_tensor_tensor(out=st32[d][i % 2][:, j, :], in0=st32[d][(i + 1) % 2][:, j, :], scalar=cdec[d],
                                                                      in1=pU[:, :], op0=ALU.mult, op1=ALU.add))
                        for d in (1, 0):
                            nx = (i + 1) % 3
                            if d == 1:
                                S.op("act", [Bst32[d][i % 2]], [Bst16[d][nx]], lambda e: e.activation(out=st16[d][nx][:], in_=st32[d][i % 2][:], func=AF.Copy))
                            else:
                                S.op("pool", [Bst32[d][i % 2]], [Bst16[d][nx]], lambda e: e.tensor_copy(out=st16[d][nx][:], in_=st32[d][i % 2][:]))
                    if i >= 1:
                        for d in (1, 0):
                            c = orders[d][i - 1]
                            cs = slice(c * 128, (c + 1) * 128)
                            cur = (i - 1) % 3
                            pB, pBb = self.PS[4 + (1 - d)], self.PSB[4 + (1 - d)]
                            for j in range(2):
                                S.op("pe", [BqT[c], Bst16[d][cur]], [pBb], lambda e: e.matmul(pB[:, :], qT[:, j, cs], st16[d][cur][:, j, :],
                                                                                              start=(j == 0), stop=(j == 1)), inc=(j == 1))
                            first = (pos[d][c] < pos[1 - d][c]) or (pos[d][c] == pos[1 - d][c] and d == 1)
                            if first:
                                S.op("act", [pBb, Bk], [Byb[c]], lambda e: e.activation(out=yb[:, c, :], in_=pB[:, :], func=AF.Copy, scale=qd[d]))
                            else:
                                S.op("dve", [pBb, Byb[c], Bk], [Byb[c]],
                                     lambda e: e.scalar_tensor_tensor(out=yb[:, c, :], in0=pB[:, :], scalar=qd[d], in1=yb[:, c, :],
                                                                      op0=ALU.mult, op1=ALU.add))

                while modq["done"] < head_target:
                    if modq["issued"] < head_target:
                        mod_issue()
                    mod_compute()

                NY = len(yy)

                def s0(c):
                    cs = slice(c * 128, (c + 1) * 128)
                    pS, pSb = self.PS[4 + c % 2], self.PSB[4 + c % 2]
                    for j in range(2):
                        S.op("pe", [BkT[c], BqT[c]], [pSb], lambda e: e.matmul(pS[:, 0:128], kT[:, j, cs], qT[:, j, cs],
                                                                               start=(j == 0), stop=(j == 1)), inc=(j == 1))

                def s1(c):
                    pS, pSb = self.PS[4 + c % 2], self.PSB[4 + c % 2]
                    S.op("dve", [pSb, Bmask], [BsT[c % 2]], lambda e: e.tensor_tensor(out=sT[c % 2][:], in0=pS[:, 0:128], in1=mask[:], op=ALU.mult))

                def s2(c):
                    pA, pAb = (self.PS[6], self.PSB[6]) if c % 2 == 0 else (self.PS[3], self.PSB[3])
                    S.op("pe", [BsT[c % 2], Bv[c]], [pAb], lambda e: e.matmul(pA[:, :], sT[c % 2][:], v[:, c, :], start=True, stop=True))

                def s3(c):
                    pA, pAb = (self.PS[6], self.PSB[6]) if c % 2 == 0 else (self.PS[3], self.PSB[3])
                    y, by = yy[c % NY], By[c % NY]
                    sst, bsst = stat[c % NY], Bstat[c % NY]
                    S.op("dve", [pAb, Byb[c]], [by], lambda e: e.tensor_tensor(out=y[:], in0=pA[:, :], in1=yb[:, c, :], op=ALU.add))
                    S.op("dve", [by], [bsst], lambda e: e.bn_stats(out=sst[:, 2:8], in_=y[:]))
                    S.op("dve", [bsst], [bsst], lambda e: e.bn_aggr(out=sst[:, 0:2], in_=sst[:, 2:8]))

                def s4(c):
                    sst, bsst = stat[c % NY], Bstat[c % NY]
                    S.op("act", [bsst, self.B_const], [bsst], lambda e: e.activation(out=sst[:, 2:3], in_=sst[:, 1:2], func=AF.Sqrt,
                                                                                     bias=self.epsc[:, 0:1], scale=1.0))

                def s5(c):
                    sst, bsst = stat[c % NY], Bstat[c % NY]
                    S.op("dve", [bsst], [bsst], lambda e: e.reciprocal(out=sst[:, 2:3], in_=sst[:, 2:3]))
                    S.op("dve", [bsst], [bsst], lambda e: e.scalar_tensor_tensor(out=sst[:, 3:4], in0=sst[:, 0:1], scalar=-1.0,
                                                                                 in1=sst[:, 2:3], op0=ALU.mult, op1=ALU.mult))

                def s6(c):
                    y, by = yy[c % NY], By[c % NY]
                    sst, bsst = stat[c % NY], Bstat[c % NY]
                    S.op("act", [by, bsst], [by], lambda e: e.activation(out=y[:], in_=y[:], func=AF.Identity,
                                                                         bias=sst[:, 3:4], scale=sst[:, 2:3]))

                def s7(c):
                    y, by = yy[c % NY], By[c % NY]
                    S.op("dve", [by, Bsgw[c]], [Bz[c % 2]], lambda e: e.tensor_tensor(out=zz[c % 2][:], in0=y[:], in1=sgw[:, c, :], op=ALU.mult))

                def s8(c):
                    z, bz = zz[c % 2], Bz[c % 2]
                    for j in range(4):
                        S.op("pe", [bz, self.B_const], [self.PTB[1]],
                             lambda e: e.transpose(self.PT[:, 512 + j * 128:512 + (j + 1) * 128], z[:, j * 128:(j + 1) * 128], self.ident[:]),
                             inc=(j == 3))

                def s9(c):
                    zt_, bzt = zTt[c % NZ], BzT[c % NZ]
                    S.op("act", [self.PTB[1]], [bzt, Bgate], lambda e: e.activation(out=zt_[:].rearrange("p c n -> p (c n)"), in_=self.PT[:, 512:1024], func=AF.Copy))
                    col = self.seq_col(c * 128)
                    S.dma("sp", self.zT[:, 4 * h:4 * h + 4, col:col + 128], zt_[:], [bzt], [self.B_zT])

                stages3 = [s0, s1, s2, s3, s4, s5, s6, s7, s8, s9]
                for it in range(NCH + len(stages3) - 1):
                    for k in reversed(range(len(stages3))):
                        if 0 <= it - k < NCH:
                            stages3[k](it - k)

    def phase_window(self):
        cfg, S = self.cfg, self.S
        KC, NQ, H = cfg.KC, cfg.NQ, cfg.HALF
        with ExitStack() as es:
            sb = lambda n, s, d: self.sb(es, n, s, d)
            xas = [sb(f"wxa{i}", [128, KC, 512], F32) for i in range(2)]
            xbs = [sb(f"wxb{i}", [128, KC, 512], F32) for i in range(2)]
            hts = [sb(f"wh{i}", [128, KC, 512], BF16) for i in range(2)]
            Bas, Bbs, Bhs = S.bufs(2), S.bufs(2), S.bufs(2)
            T = self.norm_tmps(es, "w", ntmp=8, nsq=4)
            tl = [(j0, min(512, NQ - j0)) for j0 in range(0, NQ, 512)]

            def load(it):
                j0, n = tl[it]
                S.dma("sp", xas[it % 2][:, :, :n], self.x1T[:, :, j0:j0 + n], [self.B_x1T], [Bas[it % 2]])
                S.dma("sp", xbs[it % 2][:, :, :n], self.x1T[:, :, H + j0:H + j0 + n], [self.B_x1T], [Bbs[it % 2]])

            load(0)
            for it, (j0, n) in enumerate(tl):
                if it + 1 < len(tl):
                    load(it + 1)
                xa, xb, ht = xas[it % 2], xbs[it % 2], hts[it % 2]
                Ba, Bb, Bh = Bas[it % 2], Bbs[it % 2], Bhs[it % 2]
                S.op("dve", [Bb, self.B_const], [Bb],
                     lambda e: e.tensor_scalar(out=xb[:, :, :n], in0=xb[:, :, :n], scalar1=self.rsel[:, 1:2], scalar2=None, op0=ALU.mult))
                S.op("dve", [Ba, Bb, self.B_const], [Ba],
                     lambda e: e.scalar_tensor_tensor(out=xa[:, :, :n], in0=xa[:, :, :n], scalar=self.rsel[:, 0:1], in1=xb[:, :, :n],
                                                      op0=ALU.mult, op1=ALU.add))
                S.dma("sp", self.xwT[:, :, j0:j0 + n], xa[:, :, :n], [Ba], [self.B_xwT])
                self.norm_mod(T, xa, Ba, n, ht, Bh, 1, 0, 0)
                S.dma("sp", self.hxwT[:, :, j0:j0 + n], ht[:, :, :n], [Bh], [self.B_hxwT])

    def phase_attention(self):
        cfg, S = self.cfg, self.S
        KC, AKV, NCH, NT, CTX, SEQ, KVW = cfg.KC, cfg.AKV, cfg.NCH, cfg.NT, cfg.CTX, cfg.SEQ, cfg.KVW
        scale = 1.0 / math.sqrt(128.0)
        with ExitStack() as es:
            sb = lambda n, s, d: self.sb(es, n, s, d)
            kT = sb("a_kT", [128, AKV, NT], BF16)
            vv = sb("a_v", [128, NCH, KVW], BF16)
            BkT = S.buf("a_kT")
            Bv = S.bufs(NCH, "a_v")
            qT = sb("a_qT", [128, 4, cfg.NQ], BF16)
            BqT = S.buf("a_qT")
            hxs = [sb(f"a_hx{i}", [128, KC, 512], BF16) for i in range(2)]
            Bhx = S.bufs(2)
            wk = sb("a_wk", [128, KC, KVW], BF16)
            wv = sb("a_wv", [128, KC, KVW], BF16)
            wq = [sb(f"a_wq{i}", [128, KC, 512], BF16) for i in range(2)]
            Bwk, Bwv = S.buf(), S.buf()
            Bwq = S.bufs(2)
            ropes = [sb(f"a_rope{i}", [128, 2, 512], F32) for i in range(2)]
            Brope = S.bufs(2)
            qn = sb("a_qn", [128, 1], F32)
            kn = sb("a_kn", [128, 1], F32)
            smb = sb("a_smb", [128, 1], F32)
            Bn = S.buf()
            S.dma("sp", qn[:], self.I["attQN"][:, :], [], [Bn])
            S.dma("sp", kn[:], self.I["attKN"][:, :], [], [Bn])
            S.op("dve", [], [Bn], lambda e: e.memset(smb[:], SM_BIAS))
            sq = [sb(f"a_sq{i}", [128, 512], BF16) for i in range(2)]
            Bsq = S.bufs(2)
            rstd = [sb(f"a_rstd{i}", [128, 512], F32) for i in range(2)]
            Brstd = S.bufs(2)
            nrm = [sb(f"a_nrm{i}", [128, 512], BF16) for i in range(2)]
            Bnrm = S.bufs(2)
            t1 = [sb(f"a_t1{i}", [128, 512], F32) for i in range(2)]
            Bt1 = S.bufs(2)
            t2 = [sb(f"a_t2{i}", [128, 512], F32) for i in range(2)]
            Bt2 = S.bufs(2)
            pT = [sb(f"a_pT{i}", [128, 512], BF16) for i in range(4)]
            BpT = S.bufs(4)
            rd = [sb(f"a_rd{i}", [128, 512], F32) for i in range(2)]
            Brd = S.bufs(2)
            oo = [sb(f"a_o{i}", [128, 512], BF16) for i in range(2)]
            Bo = S.bufs(2)
            S.dma("pool", wk[:], self.I["attWk"][:, :, :], [], [Bwk])
            S.dma("pool", wv[:], self.I["attWv"][:, :, :], [], [Bwv])
            cnt = dict(hx=0, rope=0, n=0)

            def load_hx(col, n, src=None, bsrc=None):
                src = self.hxT if src is None else src
                bsrc = self.B_hxT if bsrc is None else bsrc
                hx, bhx = hxs[cnt["hx"] % 2], Bhx[cnt["hx"] % 2]
                cnt["hx"] += 1
                S.dma("sp", hx[:, :, :n], src[:, :, col:col + n], [bsrc], [bhx])
                return hx, bhx

            def load_rope(t0, n, tab="ropeA"):
                rp, brp = ropes[cnt["rope"] % 2], Brope[cnt["rope"] % 2]
                cnt["rope"] += 1
                for a_ in range(2):
                    S.dma("sp", rp[:, a_, :n], self.I[tab][a_, :, t0:t0 + n], [], [brp])
                return rp, brp

            def norm_rope_group(items):
                for i, (ps, pb, n, nw, rope, dst_ap, bdst) in enumerate(items):
                    S.op("act", [pb], [Bsq[i]], lambda e: e.activation(out=sq[i][:, :n], in_=ps[:, :n], func=AF.Square))
                for i, (ps, pb, n, nw, rope, dst_ap, bdst) in enumerate(items):
                    p2, p2b = self.PS[2 + i], self.PSB[2 + i]
                    S.op("pe", [Bsq[i], self.B_const], [p2b], lambda e: e.matmul(p2[:, :n], self.ones[:], sq[i][:, :n], start=True, stop=True))
                for i, (ps, pb, n, nw, rope, dst_ap, bdst) in enumerate(items):
                    self.rstd_from(self.PS[2 + i], self.PSB[2 + i], n, rstd[i], Brstd[i], 1.0 / 128.0)
                for i, (ps, pb, n, nw, rope, dst_ap, bdst) in enumerate(items):
                    if rope is None:
                        S.op("dve", [pb, Brstd[i], Bn], [bdst],
                             lambda e: e.scalar_tensor_tensor(out=dst_ap, in0=ps[:, :n], scalar=nw, in1=rstd[i][:, :n], op0=ALU.mult, op1=ALU.mult))
                    else:
                        S.op("dve", [pb, Brstd[i], Bn], [Bnrm[i]],
                             lambda e: e.scalar_tensor_tensor(out=nrm[i][:, :n], in0=ps[:, :n], scalar=nw, in1=rstd[i][:, :n], op0=ALU.mult, op1=ALU.mult))
                for i, (ps, pb, n, nw, rope, dst_ap, bdst) in enumerate(items):
                    if rope is not None:
                        p3, p3b = self.PS[5 + i], self.PSB[5 + i]
                        S.op("pe", [Bnrm[i], self.B_const], [p3b], lambda e: e.matmul(p3[:, :n], self.protT[:], nrm[i][:, :n], start=True, stop=True))
                for i, (ps, pb, n, nw, rope, dst_ap, bdst) in enumerate(items):
                    if rope is not None:
                        rp, brp = rope
                        p3, p3b = self.PS[5 + i], self.PSB[5 + i]
                        S.op("dve", [Bnrm[i], brp], [Bt1[i]], lambda e: e.tensor_tensor(out=t1[i][:, :n], in0=nrm[i][:, :n], in1=rp[:, 0, :n], op=ALU.mult))
                        S.op("dve", [p3b, brp], [Bt2[i]], lambda e: e.tensor_tensor(out=t2[i][:, :n], in0=p3[:, :n], in1=rp[:, 1, :n], op=ALU.mult))
                        S.op("dve", [Bt1[i], Bt2[i]], [bdst], lambda e: e.tensor_tensor(out=dst_ap, in0=t1[i][:, :n], in1=t2[i][:, :n], op=ALU.add))

            tiles = []
            s0 = 0
            while s0 < NT:
                lim = CTX if s0 < CTX else NT
                n = min(512, lim - s0)
                tiles.append((s0, n))
                s0 += n
            for (s0, n) in tiles:
                hx, bhx = load_hx(self.seq_col(s0), n)
                lat = s0 >= CTX
                rope = load_rope(s0 - CTX, n) if lat else None
                for g0 in range(0, AKV, 2):
                    items = []
                    for g in range(g0, min(g0 + 2, AKV)):
                        ps, pb = self.PS[g % 2], self.PSB[g % 2]
                        for k in range(KC):
                            S.op("pe", [Bwk, bhx], [pb], lambda e: e.matmul(ps[:, :n], wk[:, k, g * 128:(g + 1) * 128], hx[:, k, :n],
                                                                            start=(k == 0), stop=(k == KC - 1)), inc=(k == KC - 1))
                        items.append((ps, pb, n, kn[:, 0:1], rope, kT[:, g, s0:s0 + n], BkT))
                    norm_rope_group(items)
                for c in range(s0 // 128, (s0 + n) // 128):
                    o = c * 128 - s0
                    ps, pb = self.PS[4], self.PSB[4]
                    for k in range(KC):
                        S.op("pe", [Bwv, bhx], [pb], lambda e: e.matmul(ps[:, :KVW], hx[:, k, o:o + 128], wv[:, k, :],
                                                                        start=(k == 0), stop=(k == KC - 1)), inc=(k == KC - 1))
                    S.op("act", [pb], [Bv[c]], lambda e: e.activation(out=vv[:, c, :], in_=ps[:, :KVW], func=AF.Copy))

            NQ = cfg.NQ
            qtiles = [(t0, min(512, NQ - t0)) for t0 in range(0, NQ, 512)]
            io = ip = 0
            for g in range(AKV):
                w, bw = wq[g % 2], Bwq[g % 2]
                S.dma("pool", w[:], self.I["attWq"][g], [], [bw])
                for (t0, n) in qtiles:
                    hx, bhx = load_hx(t0, n, self.hxwT, self.B_hxwT)
                    rope = load_rope(t0, n, "ropeAw")
                    for h0 in range(0, 4, 2):
                        items = []
                        for hq in range(h0, h0 + 2):
                            ps, pb = self.PS[hq % 2], self.PSB[hq % 2]
                            for k in range(KC):
                                S.op("pe", [bw, bhx], [pb], lambda e: e.matmul(ps[:, :n], w[:, k, hq * 128:(hq + 1) * 128], hx[:, k, :n],
                                                                               start=(k == 0), stop=(k == KC - 1)), inc=(k == KC - 1))
                            items.append((ps, pb, n, qn[:, 0:1], rope, qT[:, hq, t0:t0 + n], BqT))
                        norm_rope_group(items)
                for hq in range(4):
                    head = g * 4 + hq
                    for (t0, n) in qtiles:
                        ob = (5, 6) if io % 2 == 0 else (3, 4)
                        pO, pOb = self.PS[ob[0]], self.PSB[ob[0]]
                        pD, pDb = self.PS[ob[1]], self.PSB[ob[1]]
                        pend = {}

                        def emit_s(c):
                            nonlocal ip
                            pS, pSb = self.PS[c % 2], self.PSB[c % 2]
                            S.op("pe", [BkT, BqT], [pSb], lambda e: e.matmul(pS[:, :n], kT[:, g, c * 128:(c + 1) * 128], qT[:, hq, t0:t0 + n],
                                                                             start=True, stop=True))
                            p_, bp_ = pT[ip % 4], BpT[ip % 4]
                            ip += 1
                            S.op("act", [pSb, Bn], [bp_], lambda e: e.activation(out=p_[:, :n], in_=pS[:, :n], func=AF.Exp,
                                                                                 bias=smb[:, 0:1], scale=scale))
                            pend[c] = (p_, bp_)

                        emit_s(0)
                        for c in range(NCH):
                            if c + 1 < NCH:
                                emit_s(c + 1)
                            p_, bp_ = pend.pop(c)
                            S.op("pe", [bp_, Bv[c]], [pOb], lambda e: e.matmul(pO[:, :n], vv[:, c, g * 128:(g + 1) * 128], p_[:, :n],
                                                                               start=(c == 0), stop=(c == NCH - 1)), inc=(c == NCH - 1))
                            S.op("pe", [bp_, self.B_const], [pDb], lambda e: e.matmul(pD[:, :n], self.ones[:], p_[:, :n],
                                                                                      start=(c == 0), stop=(c == NCH - 1)), inc=True)
                        r_, br_ = rd[io % 2], Brd[io % 2]
                        S.op("dve", [pDb], [br_], lambda e: e.reciprocal(out=r_[:, :n], in_=pD[:, :n]))
                        o_, bo_ = oo[io % 2], Bo[io % 2]
                        io += 1
                        S.op("dve", [pOb, br_], [bo_], lambda e: e.tensor_tensor(out=o_[:, :n], in0=pO[:, :n], in1=r_[:, :n], op=ALU.mult))
                        S.dma("sp", self.zT[:, head, t0:t0 + n], o_[:, :n], [bo_], [self.B_zT])


def build_program(cfg, **kw):
    b = Builder(cfg, **kw)
    b.build()
    return b


def kernel(**inputs):
    cfg = Cfg()
    W = prep_weights(cfg, inputs)
    consts = make_consts(cfg)
    W.update(consts)
    n_cores = 2 * cfg.BATCH
    in_maps = []
    for i in range(n_cores):
        b, r = i // 2, i % 2
        m = dict(W)
        m.update(prep_core(cfg, inputs, b))
        m.update(prep_half(cfg, consts, r))
        in_maps.append(m)
    bld = build_program(cfg)
    res = run_bass_kernel_spmd(bld.nc, in_maps, core_ids=list(range(n_cores)))
    out = np.empty((cfg.BATCH, cfg.SEQ, cfg.D), np.float32)
    for i in range(n_cores):
        b, r = i // 2, i % 2
        oT = res.results[i]["outT"]
        out[b, r * cfg.HALF:(r + 1) * cfg.HALF] = oT.transpose(2, 1, 0).reshape(cfg.HALF, cfg.D)
    return out
```
